# Optimizing a Trainium2 kernel written in Bass

```python
import math
import jax, jax.numpy as jnp
from jax import lax
import numpy as np

D_MODEL = 1024
BATCH = 32
SEQ = 2048
DEPTH = 4

MIX_WIDTH = D_MODEL
A_WIDTH = MIX_WIDTH // 2
B_WIDTH = MIX_WIDTH // 2
C_WIDTH = MIX_WIDTH // 2
D_WIDTH = MIX_WIDTH // 2
CHUNK = 128
A_GROUPS = 4
A_GROUP_DIM = A_WIDTH // A_GROUPS
HYENA_ORDER = 2
HYENA_DIRS = 2
FILTER_BANDS = 16
FILTER_EMB = 1 + 2 * FILTER_BANDS
FILTER_HIDDEN = 64
FILTER_OUT = HYENA_DIRS * HYENA_ORDER * B_WIDTH
DECAY_TARGET = 1e-2
FAST_DECAY_PCT = 0.3
SLOW_DECAY_PCT = 1.5
POOL_WINDOWS = (2, 4, 8, 16)
C_GROUPS = len(POOL_WINDOWS)
C_GROUP_DIM = C_WIDTH // C_GROUPS
SHORT_CONV = 3
D_FF = -(-8 * D_MODEL // (3 * 256)) * 256
N_EVEN = (DEPTH + 1) // 2
N_ODD = DEPTH // 2
AB_IN = 2 * A_WIDTH + 3 * B_WIDTH
CD_IN = C_WIDTH + 3 * D_WIDTH
RMS_EPS = 1e-6
LN_EPS = 1e-5

kernel_name = "hybrid_gmlp_hyena_pool_shortconv_encoder"


def rmsnorm(x, g):
    xf = x.astype(jnp.float32)
    y = xf * lax.rsqrt(jnp.mean(xf * xf, axis=-1, keepdims=True) + RMS_EPS)
    return (y * g.astype(jnp.float32)).astype(x.dtype)


def layernorm(x, g):
    xf = x.astype(jnp.float32)
    mu = jnp.mean(xf, axis=-1, keepdims=True)
    xc = xf - mu
    y = xc * lax.rsqrt(jnp.mean(xc * xc, axis=-1, keepdims=True) + LN_EPS)
    return (y * g.astype(jnp.float32)).astype(x.dtype)


def conv3_centred(x, w):
    xp = jnp.pad(x, ((0, 0), (1, 1), (0, 0)))
    return xp[:, :-2] * w[0] + xp[:, 1:-1] * w[1] + xp[:, 2:] * w[2]


def spatial_gating(u, v, ln_g, w_s, b_s):
    bsz, L, _ = v.shape
    v = layernorm(v, ln_g)
    vc = v.reshape(bsz, L // CHUNK, CHUNK, A_GROUPS, A_GROUP_DIM)
    s = jnp.einsum("gpq,bnqgc->bnpgc", w_s, vc) + b_s.T[:, :, None]
    return u * s.reshape(bsz, L, A_WIDTH)


def hyena_filter_spectra(L, w1, b1, freq, w2, b2, w3, decay):
    t = jnp.arange(L, dtype=jnp.float32)
    t01 = t / max(L - 1, 1)
    w = 2 * math.pi * t / L
    bands = jnp.linspace(1e-4, FILTER_BANDS - 1, FILTER_BANDS, dtype=jnp.float32)
    fw = w[:, None] * bands[None, :]
    z = jnp.concatenate([t01[:, None], jnp.cos(fw), -jnp.sin(fw)], axis=-1)
    f32 = lambda a: a.astype(jnp.float32)
    h = jnp.sin(f32(freq[0]) * (z @ f32(w1) + f32(b1)))
    h = jnp.sin(f32(freq[1]) * (h @ f32(w2) + f32(b2)))
    h = h @ f32(w3)
    h = h * jnp.exp(-t01[:, None] * jnp.abs(f32(decay))[None, :])
    h = h.reshape(L, HYENA_DIRS, HYENA_ORDER, B_WIDTH)
    fwd, bwd = h[:, 0], h[:, 1]
    buf = jnp.concatenate([fwd, jnp.zeros_like(fwd[:1]), bwd[:L - 1][::-1]], axis=0)
    return jnp.fft.rfft(buf, axis=0)


def fft_long_conv(z, h_spec, skip):
    L = z.shape[1]
    zf = z.astype(jnp.float32)
    zs = jnp.fft.rfft(zf, n=2 * L, axis=1)
    y = jnp.fft.irfft(zs * h_spec[None], n=2 * L, axis=1)[:, :L]
    return (y + zf * skip.astype(jnp.float32)).astype(z.dtype)


def hyena_mixer(proj, conv_w, w1, b1, freq, w2, b2, w3, decay, skip):
    L = proj.shape[1]
    pc = conv3_centred(proj, conv_w)
    v = pc[..., :B_WIDTH]
    gates = (pc[..., B_WIDTH:2 * B_WIDTH], pc[..., 2 * B_WIDTH:])
    spec = hyena_filter_spectra(L, w1, b1, freq, w2, b2, w3, decay)
    z = v
    for o in range(HYENA_ORDER):
        z = gates[o] * fft_long_conv(z, spec[:, o], skip[o])
    return z


def multiscale_pool(x, w_grp, scale):
    L = x.shape[1]
    t = jnp.arange(L)
    outs = []
    for g, w in enumerate(POOL_WINDOWS):
        left = w // 2
        right = w - left - 1
        xg = x[..., g * C_GROUP_DIM:(g + 1) * C_GROUP_DIM].astype(jnp.float32)
        xp = jnp.pad(xg, ((0, 0), (left, right), (0, 0)))
        cs = jnp.concatenate([jnp.zeros_like(xp[:, :1]), jnp.cumsum(xp, axis=1)], axis=1)
        s = cs[:, w:] - cs[:, :L]
        cnt = (jnp.minimum(t + right, L - 1) - jnp.maximum(t - left, 0) + 1).astype(jnp.float32)
        pooled = (s / cnt[None, :, None] - xg).astype(x.dtype)
        outs.append(pooled @ w_grp[g])
    return jnp.concatenate(outs, axis=-1) * scale


def short_gated_conv(proj, conv_w):
    b = proj[..., :D_WIDTH]
    c = proj[..., D_WIDTH:2 * D_WIDTH]
    h = proj[..., 2 * D_WIDTH:]
    return b * conv3_centred(c * h, conv_w)


def swiglu(x, w_gu, w_down):
    gu = x @ w_gu
    return (jax.nn.silu(gu[..., :D_FF]) * gu[..., D_FF:]) @ w_down


def setup_inputs(seed: int = 0) -> dict:
    key = jax.random.key(seed)
    ks = jax.random.split(key, 24)

    def nrm(k, shape, scale):
        return jax.random.normal(k, shape, jnp.float32) * scale

    min_decay = math.log(DECAY_TARGET) / SLOW_DECAY_PCT
    max_decay = math.log(DECAY_TARGET) / FAST_DECAY_PCT
    base_decay = jnp.linspace(min_decay, max_decay, FILTER_OUT, dtype=jnp.float32)
    return {
        "x": nrm(ks[0], (BATCH, SEQ, D_MODEL), 1.0),
        "norm_g": 1.0 + nrm(ks[1], (DEPTH, 4, D_MODEL), 0.1),
        "ffn_w_gu": nrm(ks[2], (DEPTH, D_MODEL, 2 * D_FF), D_MODEL ** -0.5),
        "ffn_w_down": nrm(ks[3], (DEPTH, D_FF, D_MODEL), D_FF ** -0.5),
        "ab_w_in": nrm(ks[4], (N_EVEN, D_MODEL, AB_IN), D_MODEL ** -0.5),
        "ab_w_out": nrm(ks[5], (N_EVEN, A_WIDTH + B_WIDTH, D_MODEL), (A_WIDTH + B_WIDTH) ** -0.5),
        "a_ln_g": 1.0 + nrm(ks[6], (N_EVEN, A_WIDTH), 0.1),
        "a_w_s": nrm(ks[7], (N_EVEN, A_GROUPS, CHUNK, CHUNK), CHUNK ** -0.5),
        "a_b_s": 1.0 + nrm(ks[8], (N_EVEN, A_GROUPS, CHUNK), 0.1),
        "b_conv_w": nrm(ks[9], (N_EVEN, SHORT_CONV, 3 * B_WIDTH), SHORT_CONV ** -0.5),
        "b_filt_w1": nrm(ks[10], (N_EVEN, FILTER_EMB, FILTER_HIDDEN), FILTER_EMB ** -0.5),
        "b_filt_b1": nrm(ks[11], (N_EVEN, FILTER_HIDDEN), 0.1),
        "b_filt_freq": 1.0 + nrm(ks[12], (N_EVEN, 2, FILTER_HIDDEN), 0.05),
        "b_filt_w2": nrm(ks[13], (N_EVEN, FILTER_HIDDEN, FILTER_HIDDEN), FILTER_HIDDEN ** -0.5),
        "b_filt_b2": nrm(ks[14], (N_EVEN, FILTER_HIDDEN), 0.1),
        "b_filt_w3": nrm(ks[15], (N_EVEN, FILTER_HIDDEN, FILTER_OUT), 0.02),
        "b_decay": base_decay[None, :] * (1.0 + nrm(ks[16], (N_EVEN, FILTER_OUT), 0.05)),
        "b_skip": nrm(ks[17], (N_EVEN, HYENA_ORDER, B_WIDTH), 1.0),
        "cd_w_in": nrm(ks[18], (N_ODD, D_MODEL, CD_IN), D_MODEL ** -0.5),
        "cd_w_out": nrm(ks[19], (N_ODD, C_WIDTH + D_WIDTH, D_MODEL), (C_WIDTH + D_WIDTH) ** -0.5),
        "c_w": nrm(ks[20], (N_ODD, C_GROUPS, C_GROUP_DIM, C_GROUP_DIM), C_GROUP_DIM ** -0.5),
        "c_scale": 1.0 + nrm(ks[21], (N_ODD, C_WIDTH), 0.1),
        "d_conv_w": nrm(ks[22], (N_ODD, SHORT_CONV, D_WIDTH), SHORT_CONV ** -0.5),
    }


def reference(x, norm_g, ffn_w_gu, ffn_w_down, ab_w_in, ab_w_out, a_ln_g, a_w_s, a_b_s,
              b_conv_w, b_filt_w1, b_filt_b1, b_filt_freq, b_filt_w2, b_filt_b2, b_filt_w3,
              b_decay, b_skip, cd_w_in, cd_w_out, c_w, c_scale, d_conv_w):
    for i in range(DEPTH):
        g = norm_g[i]
        h = rmsnorm(x, g[0])
        if i % 2 == 0:
            j = i // 2
            proj = h @ ab_w_in[j]
            za = jax.nn.gelu(proj[..., :2 * A_WIDTH])
            y_a = spatial_gating(za[..., :A_WIDTH], za[..., A_WIDTH:], a_ln_g[j], a_w_s[j], a_b_s[j])
            y_b = hyena_mixer(proj[..., 2 * A_WIDTH:], b_conv_w[j], b_filt_w1[j], b_filt_b1[j],
                              b_filt_freq[j], b_filt_w2[j], b_filt_b2[j], b_filt_w3[j],
                              b_decay[j], b_skip[j])
            mix = jnp.concatenate([y_a, y_b], axis=-1) @ ab_w_out[j]
        else:
            j = i // 2
            proj = h @ cd_w_in[j]
            y_c = multiscale_pool(proj[..., :C_WIDTH], c_w[j], c_scale[j])
            y_d = short_gated_conv(proj[..., C_WIDTH:], d_conv_w[j])
            mix = jnp.concatenate([y_c, y_d], axis=-1) @ cd_w_out[j]
        x = x + rmsnorm(mix, g[1])
        f = swiglu(rmsnorm(x, g[2]), ffn_w_gu[i], ffn_w_down[i])
        x = x + rmsnorm(f, g[3])
    return x
```

```python
import math
import numpy as np
import ml_dtypes
from contextlib import ExitStack
import concourse.bass as bass
import concourse.mybir as mybir
from concourse.bass_utils import run_bass_kernel_spmd

F32, BF16 = mybir.dt.float32, mybir.dt.bfloat16
AF = mybir.ActivationFunctionType
ALU = mybir.AluOpType

L = 2048
D = 1024
NT = 16
DFF = 2816
NJ = 22
NCORES = 8
SEQ_PER_CORE = 4
NFFT = 4096
PI = math.pi

GELU_FUNC = AF.Gelu_apprx_tanh


class Buf:
    __slots__ = ("w", "r")

    def __init__(self):
        self.w = None
        self.r = {}


def bufs(n):
    return [Buf() for _ in range(n)]


class Ring:
    def __init__(self, kb, name, n):
        self.slots = [[kb.newsem(name), 0] for _ in range(n)]
        self.i = 0


class KB:
    EPOCH = 30000

    def __init__(self, nc, st):
        self.nc, self.st = nc, st
        self.eng = dict(pe=nc.tensor, act=nc.scalar, dve=nc.vector, pool=nc.gpsimd, sp=nc.sync)
        self.nsem = 0
        self.own = {e: set() for e in self.eng}
        self.cur = {}
        for e in self.eng:
            s = self.newsem(e)
            self.own[e].add(s)
            self.cur[e] = [s, 0]
        self.seen = {e: {} for e in self.eng}
        self.pend = {e: [] for e in self.eng}
        self.rings = []

    def newsem(self, name):
        self.nsem += 1
        return self.st.enter_context(self.nc.semaphore(f"{name}_{self.nsem}"))

    def ring(self, name, n):
        r = Ring(self, name, n)
        self.rings.append(r)
        return r

    @staticmethod
    def _deps(r, w):
        d = {}
        for b in r:
            if b.w is not None and d.get(b.w[0], 0) < b.w[1]:
                d[b.w[0]] = b.w[1]
        for b in w:
            if b.w is not None and d.get(b.w[0], 0) < b.w[1]:
                d[b.w[0]] = b.w[1]
            for s, v in b.r.items():
                if d.get(s, 0) < v:
                    d[s] = v
        return d

    def _wait(self, e, deps, skip_own):
        eng = self.eng[e]
        seen = self.seen[e]
        for s, v in deps.items():
            if seen.get(s, 0) >= v:
                continue
            if skip_own and s in self.own[e]:
                continue
            eng.wait_ge(s, v)
            seen[s] = v

    def op(self, e, fn, r=(), w=(), inc=True):
        self._wait(e, self._deps(r, w), skip_own=(e == "pe"))
        ins = fn(self.eng[e])
        if inc:
            c = self.cur[e]
            if c[1] >= self.EPOCH:
                c[0] = self.newsem(e)
                self.own[e].add(c[0])
                c[1] = 0
            ins.then_inc(c[0], 1)
            c[1] += 1
            s, v = c[0], c[1]
            for rr, ww in self.pend[e] + [(r, w)]:
                for b in rr:
                    if b.r.get(s, 0) < v:
                        b.r[s] = v
                for b in ww:
                    b.w = (s, v)
                    b.r = {}
            self.pend[e] = []
        else:
            self.pend[e].append((r, w))
        return ins

    def dma(self, q, ring, out, in_, r=(), w=()):
        slot = ring.slots[ring.i]
        ring.i = (ring.i + 1) % len(ring.slots)
        deps = self._deps(r, w)
        if slot[1] > 0 and deps.get(slot[0], 0) < slot[1]:
            deps[slot[0]] = slot[1]
        self._wait(q, deps, False)
        self.eng[q].dma_start(out=out, in_=in_).then_inc(slot[0], 16)
        slot[1] += 16
        s, v = slot[0], slot[1]
        for b in r:
            if b.r.get(s, 0) < v:
                b.r[s] = v
        for b in w:
            b.w = (s, v)
            b.r = {}

    def barrier(self):
        ids = {}
        for e in self.eng:
            assert not self.pend[e]
            c = self.cur[e]
            if c[1] > 0:
                ids[c[0]] = c[1]
        for rg in self.rings:
            for s, v in rg.slots:
                if v > 0:
                    ids[s] = v
        for e in self.eng:
            if e in ("pe", "sp"):
                continue
            self._wait(e, dict(ids), skip_own=True)

    def finish(self):
        for e in self.eng:
            assert not self.pend[e]
        for rg in self.rings:
            for s, v in rg.slots:
                if v > 0:
                    self.nc.sync.wait_ge(s, v)


_CONST = {}


def _consts():
    if _CONST:
        return _CONST
    bf = ml_dtypes.bfloat16
    N = NFFT
    f = np.arange(L, dtype=np.float64)
    om = 2 * np.pi * (f + 0.5) / N
    s = np.arange(L, dtype=np.float64)
    ang = np.outer(s, om)
    FzT = np.stack([np.cos(ang), -np.sin(ang)], 0)
    fz = FzT.reshape(2, 16, 128, 16, 128).transpose(3, 2, 0, 1, 4)
    _CONST["fz"] = np.ascontiguousarray(fz).reshape(16, 128, 4096).astype(bf)
    idxb = (N - 1 - s)
    angb = np.outer(idxb, om)
    mask = (s < L - 1).astype(np.float64)[:, None]
    FhT_re = np.concatenate([np.cos(ang), -np.cos(angb) * mask], 0)
    FhT_im = np.concatenate([-np.sin(ang), np.sin(angb) * mask], 0)
    FhT = np.stack([FhT_re, FhT_im], 0)
    fh = FhT.reshape(2, 32, 128, 16, 128).transpose(3, 0, 2, 1, 4)
    _CONST["fh"] = np.ascontiguousarray(fh).reshape(32, 128, 4096).astype(bf)
    del FhT, FhT_re, FhT_im, fh
    t = np.arange(L, dtype=np.float64)
    angt = np.outer(om, t)
    GT = np.concatenate([np.cos(angt), -np.sin(angt)], 0) * (2.0 / N)
    g = GT.reshape(4, 8, 128, 4, 512).transpose(3, 0, 2, 1, 4)
    _CONST["ginv"] = np.ascontiguousarray(g).reshape(16, 128, 4096).astype(bf)
    tt = np.arange(L, dtype=np.float32)
    t01 = tt / np.float32(L - 1)
    w = (np.float32(2 * math.pi) * tt / np.float32(L)).astype(np.float32)
    bands = np.linspace(1e-4, 15, 16, dtype=np.float32)
    fw = w[:, None] * bands[None, :]
    z = np.concatenate([t01[:, None], np.cos(fw), -np.sin(fw)], -1).astype(np.float32)
    _CONST["zT"] = np.ascontiguousarray(z.T)
    _CONST["negt01"] = np.ascontiguousarray((-t01).reshape(16, 128).T)
    ic = np.zeros((4, 16), np.float32)
    for gi, wv in enumerate((2, 4, 8, 16)):
        left = wv // 2
        right = wv - left - 1
        for k in range(8):
            for pos, tpos in ((k, k), (8 + k, L - 8 + k)):
                cnt = min(tpos + right, L - 1) - max(tpos - left, 0) + 1
                ic[gi, pos] = 1.0 / cnt
    _CONST["icnt"] = np.ascontiguousarray(np.broadcast_to(ic.reshape(1, 64), (128, 64)))
    _CONST["ident"] = np.eye(128, dtype=np.float32).astype(bf)
    _CONST["identf"] = np.eye(128, dtype=np.float32)
    return _CONST


CONST_SPECS = dict(fz=([16, 128, 4096], BF16), fh=([32, 128, 4096], BF16), ginv=([16, 128, 4096], BF16),
                   zT=([33, 2048], F32), negt01=([128, 16], F32), icnt=([128, 64], F32),
                   ident=([128, 128], BF16), identf=([128, 128], F32))

W_SPECS = dict(
    norm_g=[4, 4, 1024], ffn_w_gu=[4, 1024, 5632], ffn_w_down=[4, 2816, 1024],
    ab_w_in=[2, 1024, 2560], ab_w_out=[2, 1024, 1024], a_ln_g=[2, 512], a_w_s=[2, 4, 128, 128],
    a_b_s=[2, 4, 128], b_conv_w=[2, 3, 1536], b_filt_w1=[2, 33, 64], b_filt_b1=[2, 64],
    b_filt_freq=[2, 2, 64], b_filt_w2=[2, 64, 64], b_filt_b2=[2, 64], b_filt_w3=[2, 64, 2048],
    b_decay=[2, 2048], b_skip=[2, 2, 512], cd_w_in=[2, 1024, 2048], cd_w_out=[2, 1024, 1024],
    c_w=[2, 4, 128, 128], c_scale=[2, 512], d_conv_w=[2, 3, 512])


def build(nseq=SEQ_PER_CORE, plan=None):
    if plan is None:
        plan = [("filt", 0), ("filt", 1)]
        for i in range(4):
            plan += [("even" if i % 2 == 0 else "odd", i), ("ffn", i)]
    nc = bass.Bass("TRN2", target_bir_lowering=False)
    xin = nc.dram_tensor("x", [nseq, L, D], F32, kind="ExternalInput").ap()
    yout = nc.dram_tensor("y", [nseq, L, D], F32, kind="ExternalOutput").ap()
    Wd = {k: nc.dram_tensor(k, s, F32, kind="ExternalInput").ap() for k, s in W_SPECS.items()}
    Cd = {k: nc.dram_tensor(k, s, dt, kind="ExternalInput").ap() for k, (s, dt) in CONST_SPECS.items()}
    Hd = nc.dram_tensor("Hscr", [2, 16, 128, 2048], F32).ap()
    HdB = [bufs(16) for _ in range(2)]

    with ExitStack() as st:
        kb = KB(nc, st)
        sb = lambda n, s, d: st.enter_context(nc.sbuf_tensor("s_" + n, s, d))
        ident = sb("ident", [128, 128], BF16)
        identf = sb("identf", [128, 128], F32)
        regA = sb("regA", [128, 16400], BF16)
        R = sb("R", [128, 47104], BF16)
        wb = [sb(f"wb{i}", [128, 4096], BF16) for i in range(3)]
        xt = [sb(f"xt{i}", [128, 1024], F32) for i in range(3)]
        xs = [sb(f"xs{i}", [128, 1024], BF16) for i in range(3)]
        junk = sb("junk", [128, 1024], BF16)
        Hb = [sb(f"Hb{i}", [128, 2, 512], F32) for i in range(2)]
        gb = [sb(f"gb{i}", [128, 1024], F32) for i in range(2)]
        tmpA = [sb(f"tmpA{i}", [128, 512], F32) for i in range(2)]
        tmpB = [sb(f"tmpB{i}", [128, 512], F32) for i in range(2)]
        stat = sb("stat", [128, 64], F32)
        bn6 = sb("bn6", [128, 8], F32)
        cols = sb("cols", [128, 64], F32)
        rows = sb("rows", [64, 128], F32)
        lnb = sb("lnb", [128, 512], F32)
        bsb = sb("bsb", [128, 512], F32)
        wsT = sb("wsT", [128, 512], BF16)
        wsN = sb("wsN", [128, 512], BF16)
        cwg = sb("cwg", [128, 512], BF16)
        pcb2 = sb("pcb2", [128, 2048], BF16)
        icnt = sb("icnt", [128, 64], F32)
        negt01 = sb("negt01", [128, 16], F32)
        B = dict(ident=Buf(), regA=bufs(16), Y=bufs(32), wb=bufs(3), xt=bufs(3), xs=bufs(3), Hb=bufs(2),
                 gb=bufs(2), tmpA=bufs(2), tmpB=bufs(2), stat=Buf(), bn6=Buf(), cols=Buf(), rows=Buf(),
                 lnb=Buf(), bsb=Buf(), wsT=Buf(), wsN=Buf(), cwg=Buf(), misc=Buf())
        PS = [st.enter_context(nc.psum_tensor(f"ps{i}", [128, 1024], F32)) for i in range(4)]
        PB = bufs(8)
        statR = bufs(4)
        pnS = bufs(4)
        pstate = {"b": 0}

        def bank(i):
            return PS[i // 2][:, (i % 2) * 512:(i % 2) * 512 + 512]

        def nextbank():
            i = pstate["b"]
            pstate["b"] = (i + 1) % 8
            return i

        def nextpair():
            i = pstate["b"]
            if i % 2:
                i = (i + 1) % 8
            pstate["b"] = (i + 2) % 8
            return i // 2

        rg_x = kb.ring("x", 4)
        rg_st = kb.ring("st", 4)
        rg_w = kb.ring("w", 4)
        rg_c = kb.ring("c", 4)
        rg_h = kb.ring("h", 2)
        rg_m = kb.ring("m", 4)
        rg_pst = kb.ring("pst", 4)

        YB = [[bufs(NT) for _ in range(nseq)]][0]

        hT = regA[:, :].rearrange("p (k t) -> p k t", k=8)
        Yv = regA[:, 0:16384].rearrange("p (f c) -> p f c", c=512)

        def Rv(off, n):
            return R[:, off:off + n]

        def Rf(off, n):
            return R[:, off:off + 2 * n].bitcast(F32)

        kb.dma("sp", rg_m, ident[:], Cd["ident"], w=[B["ident"]])
        kb.dma("sp", rg_m, identf[:], Cd["identf"], w=[B["ident"]])
        kb.dma("sp", rg_m, icnt[:], Cd["icnt"], w=[B["misc"]])
        kb.dma("sp", rg_m, negt01[:], Cd["negt01"], w=[B["misc"]])
        kb.op("dve", lambda e: e.memset(regA[:, :], 0.0), w=B["regA"] + B["Y"])
        RB = Buf()
        kb.op("pool", lambda e: e.memset(R[:, :], 0.0), w=[RB])
        for s in range(nseq):
            for h in range(4):
                kb.dma("sp", rg_st, yout[s, h * 512:(h + 1) * 512, :], xin[s, h * 512:(h + 1) * 512, :],
                       w=YB[s][h * 4:(h + 1) * 4])

        def load_bcast(dst, src_row, buf, n):
            kb.dma("sp", rg_m, dst[:, 0:n], src_row.partition_broadcast(128), w=[buf])

        def load_cols(src2d, nrows, col0):
            kb.dma("sp", rg_m, rows[0:nrows, :], src2d, w=[B["rows"]])
            b = nextbank()
            kb.op("pe", lambda e: e.transpose(out=bank(b)[:, 0:nrows], in_=rows[0:nrows, :],
                                              identity=identf[0:nrows, 0:nrows]),
                  r=[B["rows"], B["ident"]], w=[PB[b]])
            kb.op("dve", lambda e: e.tensor_copy(out=cols[:, col0:col0 + nrows], in_=bank(b)[:, 0:nrows]),
                  r=[PB[b]], w=[B["cols"]])

        def rstd_from_ss(ss_ap, std_ap, rstd_ap, scale, eps, rbufs, wbufs):
            kb.op("act", lambda e: e.activation(out=std_ap, in_=ss_ap, func=AF.Sqrt, scale=scale, bias=eps),
                  r=rbufs, w=wbufs)
            kb.op("dve", lambda e: e.reciprocal(out=rstd_ap, in_=std_ap), r=wbufs, w=wbufs)

        def norm_T(s, gidx, layer):
            gi = gidx % 2
            load_bcast(gb[gi], Wd["norm_g"][layer, gidx, :], B["gb"][gi], 1024)
            kb.op("dve", lambda e: e.memset(hT[:, :, 0:1], 0.0), w=B["regA"] + B["Y"])
            kb.op("dve", lambda e: e.memset(hT[:, :, 2049:2050], 0.0), w=B["regA"] + B["Y"])
            for n in range(NT):
                xi = n % 3
                sc0 = 32 + (n % 4) * 4
                sB = [statR[n % 4]]
                kb.dma("sp", rg_x, xt[xi][:], yout[s, n * 128:(n + 1) * 128, :], r=[YB[s][n]], w=[B["xt"][xi]])
                kb.op("act", lambda e: e.activation(out=junk[:], in_=xt[xi][:], func=AF.Square,
                                                    accum_out=stat[:, sc0:sc0 + 1]), r=[B["xt"][xi]], w=sB)
                rstd_from_ss(stat[:, sc0:sc0 + 1], stat[:, sc0 + 1:sc0 + 2], stat[:, sc0 + 2:sc0 + 3], 1.0 / D, 1e-6,
                             sB, sB)
                norm_tail(xi, n, gi, sc0, sB)

        tails = []

        def norm_tail(xi, n, gi, sc0, sB, defer=False):
            si = n % 3
            kb.op("dve", lambda e: e.scalar_tensor_tensor(out=xs[si][:], in0=xt[xi][:],
                                                          scalar=stat[:, sc0 + 2:sc0 + 3],
                                                          in1=gb[gi][:], op0=ALU.mult, op1=ALU.mult),
                  r=[B["xt"][xi], B["gb"][gi]] + sB, w=[B["xs"][si]])

            def tail():
                b = nextbank()
                pT = bank(b).bitcast(BF16).rearrange("p (k t) -> p k t", k=8)
                for k in range(8):
                    kb.op("pe", lambda e: e.transpose(out=pT[:, k, :], in_=xs[si][:, k * 128:(k + 1) * 128],
                                                      identity=ident[:]),
                          r=[B["xs"][si], B["ident"]], w=[PB[b]], inc=(k == 7))
                kb.op("act", lambda e: e.copy(out=hT[:, :, 1 + n * 128:1 + (n + 1) * 128], in_=pT),
                      r=[PB[b]], w=[B["regA"][n]])
            if defer:
                tails.append(tail)
            else:
                tail()

        def flush_tails(keep=0):
            while len(tails) > keep:
                tails.pop(0)()

        pn_state = {"i": 0}

        def postnorm(pp, s, n, gi, fuse_next=False, nsrc=None):
            P2 = PS[pp][:, :]
            xi = pn_state["i"] % 3
            sl = pn_state["i"] % 4
            pn_state["i"] += 1
            sc = 16 + sl * 4
            sB = [pnS[sl]]
            ta = [tmpA[0], tmpA[1]]
            kb.dma("sp", rg_x, xt[xi][:], yout[s, n * 128:(n + 1) * 128, :], r=[YB[s][n]], w=[B["xt"][xi]])
            kb.op("act", lambda e: e.activation(out=junk[:], in_=P2, func=AF.Square, accum_out=stat[:, sc:sc + 1]),
                  r=[PB[2 * pp], PB[2 * pp + 1]], w=sB)
            rstd_from_ss(stat[:, sc:sc + 1], stat[:, sc + 1:sc + 2], stat[:, sc + 2:sc + 3], 1.0 / D, 1e-6, sB, sB)
            for hf in range(2):
                kb.op("dve", lambda e: e.tensor_tensor(out=ta[hf][:], in0=P2[:, hf * 512:(hf + 1) * 512],
                                                       in1=gb[gi][:, hf * 512:(hf + 1) * 512], op=ALU.mult),
                      r=[PB[2 * pp + hf], B["gb"][gi]], w=[B["tmpA"][hf]])
            for hf in range(2):
                kb.op("dve", lambda e: e.scalar_tensor_tensor(out=xt[xi][:, hf * 512:(hf + 1) * 512], in0=ta[hf][:],
                                                              scalar=stat[:, sc + 2:sc + 3],
                                                              in1=xt[xi][:, hf * 512:(hf + 1) * 512],
                                                              op0=ALU.mult, op1=ALU.add),
                      r=[B["tmpA"][hf]] + sB, w=[B["xt"][xi]])
            kb.dma("pool", rg_pst, yout[s, n * 128:(n + 1) * 128, :], xt[xi][:], r=[B["xt"][xi]], w=[YB[s][n]])
            if fuse_next:
                sc0 = 32 + (n % 4) * 4
                nB = [statR[n % 4]]
                if nsrc is not None:
                    xi = pn_state["i"] % 3
                    pn_state["i"] += 1
                    kb.dma("sp", rg_x, xt[xi][:], yout[nsrc, n * 128:(n + 1) * 128, :], r=[YB[nsrc][n]],
                           w=[B["xt"][xi]])
                kb.op("act", lambda e: e.activation(out=junk[:], in_=xt[xi][:], func=AF.Square,
                                                    accum_out=stat[:, sc0:sc0 + 1]), r=[B["xt"][xi]], w=nB)
                rstd_from_ss(stat[:, sc0:sc0 + 1], stat[:, sc0 + 1:sc0 + 2], stat[:, sc0 + 2:sc0 + 3], 1.0 / D, 1e-6,
                             nB, nB)
                norm_tail(xi, n, 0, sc0, nB, defer=True)

        def prep_next(nxt):
            if nxt is not None:
                load_bcast(gb[0], Wd["norm_g"][nxt[0], nxt[1], :], B["gb"][0], 1024)

        def load_wblock(i, src):
            shp = src.shape
            dst = wb[i][:, 0:shp[1] * shp[2]].rearrange("p (k c) -> p k c", k=shp[1])
            kb.dma("pool", rg_w, dst, src, w=[B["wb"][i]])
            return dst

        wstate = {"i": 0}

        def next_wb():
            i = wstate["i"]
            wstate["i"] = (i + 1) % 3
            return i

        pre = {}

        def get_wblock(key, src):
            if key in pre:
                return pre.pop(key)
            wi_ = next_wb()
            return wi_, load_wblock(wi_, src)

        def get_wgu(layer, jb, key):
            if key in pre:
                return pre.pop(key)
            wgu_ = Wd["ffn_w_gu"][layer].rearrange("(k p) c -> p k c", p=128)
            wi_ = next_wb()
            wv_ = wb[wi_][:, :].rearrange("p (k g c) -> p k g c", k=8, g=2)
            kb.dma("pool", rg_w, wv_[:, :, 0, :], wgu_[:, :, jb * 256:(jb + 1) * 256], w=[B["wb"][wi_]])
            kb.dma("pool", rg_w, wv_[:, :, 1, :], wgu_[:, :, DFF + jb * 256:DFF + (jb + 1) * 256],
                   w=[B["wb"][wi_]])
            return wi_, wv_

        def prefetch(nst, idx):
            if nst is None:
                return
            kind, layer = nst
            if kind == "ffn":
                key = ("wgu", layer, 0, idx)
                pre[key] = get_wgu(layer, idx, key)
            elif kind == "even":
                win_ = Wd["ab_w_in"][layer // 2].rearrange("(k p) c -> p k c", p=128)
                key = ("ein", layer, idx)
                pre[key] = get_wblock(key, win_[:, :, idx * 512:(idx + 1) * 512])
            elif kind == "odd" and idx == 0:
                win_ = Wd["cd_w_in"][layer // 2].rearrange("(k p) c -> p k c", p=128)
                key = ("oin", layer, 0)
                pre[key] = get_wblock(key, win_[:, :, 0:512])

        def out_proj(s, wout, aT, bT, aB, bBufs, layer, nxt=None, nst=None):
            load_bcast(gb[1], Wd["norm_g"][layer, 1, :], B["gb"][1], 1024)
            prep_next(nxt)
            if nxt is not None:
                kb.op("dve", lambda e: e.memset(hT[:, :, 0:1], 0.0), w=B["regA"] + B["Y"])
                kb.op("dve", lambda e: e.memset(hT[:, :, 2049:2050], 0.0), w=B["regA"] + B["Y"])
            wv = wout.rearrange("(k p) c -> p k c", p=128)
            wi = [next_wb(), next_wb()]
            wt = [load_wblock(wi[c], wv[:, :, c * 512:(c + 1) * 512]) for c in range(2)]
            prefetch(nst, 0)
            for n in range(NT):
                pp = nextpair()
                for c in range(2):
                    for k in range(8):
                        src = aT if k < 4 else bT
                        kb.op("pe", lambda e: e.matmul(bank(2 * pp + c), lhsT=src[:, k % 4, n * 128:(n + 1) * 128],
                                                       rhs=wt[c][:, k, :], start=(k == 0), stop=(k == 7)),
                              r=[aB, bBufs, B["wb"][wi[c]]], w=[PB[2 * pp + c]], inc=(k == 7))
                flush_tails(keep=1)
                postnorm(pp, s, n, 1, fuse_next=(nxt is not None))
            flush_tails()
            prefetch(nst, 1)

        def ffn(s, layer, pre_norm=True, nxt=None, nst=None, nsrc=None):
            if pre_norm:
                norm_T(s, 2, layer)
            load_bcast(gb[1], Wd["norm_g"][layer, 3, :], B["gb"][1], 1024)
            wgu = Wd["ffn_w_gu"][layer].rearrange("(k p) c -> p k c", p=128)
            wdn = Wd["ffn_w_down"][layer].rearrange("(j p) c -> p j c", p=128)
            hff = Rv(0, 22528).rearrange("p (j t) -> p j t", j=NJ)
            wd = Rv(22528, 22528).rearrange("p (j c) -> p j c", j=NJ)
            hffB, wdB = Buf(), Buf()
            for half in range(2):
                for jb in range(NJ // 2):
                    wi, wv = get_wgu(layer, jb, ("wgu", layer, half, jb))
                    if half == 0:
                        kb.dma("pool", rg_w, wd[:, 2 * jb:2 * jb + 2, :], wdn[:, 2 * jb:2 * jb + 2, :], r=[RB],
                               w=[wdB])
                    for jj in range(2):
                        j = jb * 2 + jj
                        for tg in range(2):
                            t0 = 1 + half * 1024 + tg * 512
                            bg, bu = nextbank(), nextbank()
                            for g_, bb in ((0, bg), (1, bu)):
                                for k in range(8):
                                    kb.op("pe", lambda e: e.matmul(bank(bb), lhsT=wv[:, k, g_, jj * 128:(jj + 1) * 128],
                                                                   rhs=hT[:, k, t0:t0 + 512], start=(k == 0),
                                                                   stop=(k == 7)),
                                          r=[B["wb"][wi]] + B["regA"][half * 8:half * 8 + 8], w=[PB[bb]],
                                          inc=(k == 7))
                            ti = (j * 2 + tg) % 2
                            kb.op("act", lambda e: e.activation(out=tmpB[ti][:], in_=bank(bg), func=AF.Silu),
                                  r=[PB[bg]], w=[B["tmpB"][ti]])
                            kb.op("dve", lambda e: e.tensor_tensor(out=hff[:, j, tg * 512:(tg + 1) * 512],
                                                                   in0=bank(bu), in1=tmpB[ti][:], op=ALU.mult),
                                  r=[PB[bu], B["tmpB"][ti], RB], w=[hffB])
                if half == 0:
                    prep_next(nxt)
                    for jb_ in range(2):
                        key_ = ("wgu", layer, 1, jb_)
                        pre[key_] = get_wgu(layer, jb_, key_)
                else:
                    prefetch(nst, 0)
                    prefetch(nst, 1)
                for nn in range(8):
                    n = half * 8 + nn
                    pp = nextpair()
                    for c in range(2):
                        for j in range(NJ):
                            kb.op("pe", lambda e: e.matmul(bank(2 * pp + c), lhsT=hff[:, j, nn * 128:(nn + 1) * 128],
                                                           rhs=wd[:, j, c * 512:(c + 1) * 512], start=(j == 0),
                                                           stop=(j == NJ - 1)),
                                  r=[hffB, wdB], w=[PB[2 * pp + c]], inc=(j == NJ - 1))
                    flush_tails(keep=0)
                    postnorm(pp, s, n, 1, fuse_next=(nxt is not None), nsrc=nsrc)
                flush_tails()

        def mixer_odd(s, layer, pre_norm=True, nxt=None, nst=None):
            j = layer // 2
            if pre_norm:
                norm_T(s, 0, layer)
            load_cols(Wd["c_scale"][j].rearrange("(r p) -> r p", p=128), 4, 0)
            load_cols(Wd["d_conv_w"][j].rearrange("k (c p) -> (k c) p", p=128), 12, 4)
            kb.dma("pool", rg_w, cwg[:, :].rearrange("p (g c) -> p g c", g=4),
                   Wd["c_w"][j].rearrange("g c d -> c g d"), w=[B["cwg"]])
            cwv = cwg[:, :].rearrange("p (g c) -> p g c", g=4)
            win = Wd["cd_w_in"][j].rearrange("(k p) c -> p k c", p=128)
            ycT = Rv(0, 8192).rearrange("p (c t) -> p c t", c=4)
            ydT = Rv(8192, 8192).rearrange("p (c t) -> p c t", c=4)
            o = 16384
            Pp = Rf(o, 2080); o += 4160
            Aa = Rf(o, 2080); o += 4160
            Ab = Rf(o, 2080); o += 4160
            pl = Rv(o, 2048); o += 2048
            ccs = Rf(o, 2048); o += 4096
            qq = Rf(o, 2050); o += 4100
            acc = Rf(o, 2048); o += 4096
            assert o <= 47104
            ycB, ydB, PpB, AB_, plB, ccB, qB, accB = bufs(8)
            kb.op("dve", lambda e: e.memset(Pp[:, 0:16], 0.0), r=[RB], w=[PpB])
            kb.op("dve", lambda e: e.memset(Pp[:, 2064:2080], 0.0), r=[RB], w=[PpB])
            kb.op("dve", lambda e: e.memset(qq[:, 0:1], 0.0), r=[RB], w=[qB])
            kb.op("dve", lambda e: e.memset(qq[:, 2049:2050], 0.0), r=[RB], w=[qB])
            w0, wt0 = get_wblock(("oin", layer, 0), win[:, :, 0:512])

            def c_proj(g):
                for tg in range(4):
                    b = nextbank()
                    for k in range(8):
                        kb.op("pe", lambda e: e.matmul(bank(b), lhsT=wt0[:, k, g * 128:(g + 1) * 128],
                                                       rhs=hT[:, k, 1 + tg * 512:1 + (tg + 1) * 512],
                                                       start=(k == 0), stop=(k == 7)),
                              r=[B["wb"][w0]] + B["regA"], w=[PB[b]], inc=(k == 7))
                    kb.op("act", lambda e: e.copy(out=Pp[:, 16 + tg * 512:16 + (tg + 1) * 512], in_=bank(b)),
                          r=[PB[b], RB], w=[PpB])

            def c_chain(g):
                wv_ = (2, 4, 8, 16)[g]
                left = wv_ // 2
                src, dst_list = Pp, [Aa, Ab]
                sh = 1
                li = 0
                while sh < wv_:
                    dst = dst_list[li % 2]
                    n_ = 2080 - sh
                    s_ = src
                    kb.op("dve", lambda e: e.tensor_tensor(out=dst[:, 0:n_], in0=s_[:, 0:n_], in1=s_[:, sh:sh + n_],
                                                            op=ALU.add), r=[PpB, AB_], w=[AB_])
                    src = dst
                    sh *= 2
                    li += 1
                a_ = src
                kb.op("dve", lambda e: e.scalar_tensor_tensor(out=pl[:, :], in0=a_[:, 16 - left:16 - left + 2048],
                                                              scalar=1.0 / wv_, in1=Pp[:, 16:16 + 2048],
                                                              op0=ALU.mult, op1=ALU.subtract),
                      r=[AB_, PpB], w=[plB])
                for (c0, t0) in ((0, 0), (8, L - 8)):
                    kb.op("dve", lambda e: e.tensor_tensor(out=bn6[:, 0:8],
                                                           in0=a_[:, 16 - left + t0:16 - left + t0 + 8],
                                                           in1=icnt[:, g * 16 + c0:g * 16 + c0 + 8], op=ALU.mult),
                          r=[AB_, B["misc"]], w=[B["bn6"]])
                    kb.op("dve", lambda e: e.tensor_tensor(out=pl[:, t0:t0 + 8], in0=bn6[:, 0:8],
                                                           in1=Pp[:, 16 + t0:16 + t0 + 8], op=ALU.subtract),
                          r=[B["bn6"], PpB], w=[plB])

            def c_mix(g):
                for tg in range(4):
                    b = nextbank()
                    kb.op("pe", lambda e: e.matmul(bank(b), lhsT=cwv[:, g, :], rhs=pl[:, tg * 512:(tg + 1) * 512],
                                                   start=True, stop=True), r=[B["cwg"], plB], w=[PB[b]])
                    kb.op("act", lambda e: e.activation(out=ycT[:, g, tg * 512:(tg + 1) * 512], in_=bank(b),
                                                        func=AF.Copy, scale=cols[:, g:g + 1]),
                          r=[PB[b], B["cols"], RB], w=[ycB])

            c_proj(0)
            for g in range(4):
                c_chain(g)
                if g + 1 < 4:
                    c_proj(g + 1)
                c_mix(g)
            wis = [next_wb(), next_wb(), next_wb()]
            wts = [load_wblock(wis[q], win[:, :, 512 * (q + 1):512 * (q + 2)]) for q in range(3)]
            for c in range(4):
                def proj(q, tg):
                    b = nextbank()
                    for k in range(8):
                        kb.op("pe", lambda e: e.matmul(bank(b), lhsT=wts[q][:, k, c * 128:(c + 1) * 128],
                                                       rhs=hT[:, k, 1 + tg * 512:1 + (tg + 1) * 512],
                                                       start=(k == 0), stop=(k == 7)),
                              r=[B["wb"][wis[q]]] + B["regA"], w=[PB[b]], inc=(k == 7))
                    return b
                for tg in range(4):
                    b = proj(1, tg)
                    kb.op("act", lambda e: e.copy(out=ccs[:, tg * 512:(tg + 1) * 512], in_=bank(b)),
                          r=[PB[b], RB], w=[ccB])
                for tg in range(4):
                    b = proj(2, tg)
                    kb.op("dve", lambda e: e.tensor_tensor(out=qq[:, 1 + tg * 512:1 + (tg + 1) * 512], in0=bank(b),
                                                           in1=ccs[:, tg * 512:(tg + 1) * 512], op=ALU.mult),
                          r=[PB[b], ccB, RB], w=[qB])
                kb.op("act", lambda e: e.activation(out=acc[:, :], in_=qq[:, 0:2048], func=AF.Copy,
                                                    scale=cols[:, 4 + c:5 + c]), r=[qB, B["cols"]], w=[accB])
                for kk in (1, 2):
                    kb.op("dve", lambda e: e.scalar_tensor_tensor(out=acc[:, :], in0=qq[:, kk:kk + 2048],
                                                                  scalar=cols[:, 4 + kk * 4 + c:5 + kk * 4 + c],
                                                                  in1=acc[:, :], op0=ALU.mult, op1=ALU.add),
                          r=[qB, B["cols"]], w=[accB])
                for tg in range(4):
                    b = proj(0, tg)
                    kb.op("dve", lambda e: e.tensor_tensor(out=ydT[:, c, tg * 512:(tg + 1) * 512], in0=bank(b),
                                                           in1=acc[:, tg * 512:(tg + 1) * 512], op=ALU.mult),
                          r=[PB[b], accB], w=[ydB])
            out_proj(s, Wd["cd_w_out"][j], ycT, ydT, ycB, ydB, layer, nxt, nst)

        def filters(j):
            hbuf = Rv(0, 32768).rearrange("p (r c) -> p r c", r=32)
            o = 32768
            zT = Rf(o, 2048); o += 4096
            h1 = Rf(o, 2048); o += 4096
            h2 = Rf(o, 2048); o += 4096
            arg = Rf(o, 512); o += 1024
            msk = Rf(o, 512); o += 1024
            assert o <= 47104
            hbB, zB, h1B, h2B, argB, w3B, decB, skB = bufs(8)
            w1 = tmpA[0]
            w2 = tmpA[1]
            w3 = regA[:, 0:4096].bitcast(F32)
            dec = regA[:, 4096:8192].bitcast(F32)
            skb = regA[:, 8192:10240].bitcast(F32)
            Hs = regA[:, 10240:14336].bitcast(F32)
            HsB = Buf()
            kb.dma("sp", rg_m, zT[0:33, :], Cd["zT"], r=[RB], w=[zB])
            kb.dma("sp", rg_m, w1[0:33, 0:64], Wd["b_filt_w1"][j], w=[B["tmpA"][0]])
            kb.dma("sp", rg_m, w2[0:64, 0:64], Wd["b_filt_w2"][j], w=[B["tmpA"][1]])
            kb.dma("sp", rg_m, w3[0:64, :], Wd["b_filt_w3"][j], r=B["regA"] + B["Y"], w=[w3B])
            kb.dma("sp", rg_m, dec, Wd["b_decay"][j].partition_broadcast(128), r=B["regA"] + B["Y"], w=[decB])
            kb.dma("sp", rg_m, skb, Wd["b_skip"][j].rearrange("o c -> (o c)").partition_broadcast(128),
                   r=B["regA"] + B["Y"], w=[skB])
            kb.dma("sp", rg_m, cols[0:64, 32:33], Wd["b_filt_b1"][j].rearrange("(p o) -> p o", o=1), w=[B["cols"]])
            kb.dma("sp", rg_m, cols[0:64, 33:34], Wd["b_filt_freq"][j, 0].rearrange("(p o) -> p o", o=1),
                   w=[B["cols"]])
            kb.dma("sp", rg_m, cols[0:64, 34:35], Wd["b_filt_b2"][j].rearrange("(p o) -> p o", o=1), w=[B["cols"]])
            kb.dma("sp", rg_m, cols[0:64, 35:36], Wd["b_filt_freq"][j, 1].rearrange("(p o) -> p o", o=1),
                   w=[B["cols"]])
            kb.op("dve", lambda e: e.tensor_tensor(out=cols[0:64, 36:37], in0=cols[0:64, 32:33], in1=cols[0:64, 33:34],
                                                   op=ALU.mult), r=[B["cols"]], w=[B["cols"]])
            kb.op("dve", lambda e: e.tensor_tensor(out=cols[0:64, 37:38], in0=cols[0:64, 34:35], in1=cols[0:64, 35:36],
                                                   op=ALU.mult), r=[B["cols"]], w=[B["cols"]])
            kb.op("act", lambda e: e.activation(out=dec, in_=dec, func=AF.Abs), r=[decB], w=[decB])

            def sin_layer(wt, kdim, src, srcB, dst, dstB, fcol, fbcol, wtB):
                for blk in range(4):
                    b = nextbank()
                    kb.op("pe", lambda e: e.matmul(bank(b)[0:64, :], lhsT=wt[0:kdim, 0:64],
                                                   rhs=src[0:kdim, blk * 512:(blk + 1) * 512], start=True, stop=True),
                          r=[wtB, srcB], w=[PB[b]])
                    a = arg[0:64, :]
                    m = msk[0:64, :]
                    kb.op("dve", lambda e: e.tensor_scalar(out=a, in0=bank(b)[0:64, :], scalar1=cols[0:64, fcol:fcol + 1],
                                                           scalar2=cols[0:64, fbcol:fbcol + 1], op0=ALU.mult,
                                                           op1=ALU.add), r=[PB[b], B["cols"]], w=[argB])
                    kb.op("dve", lambda e: e.tensor_scalar(out=m, in0=a, scalar1=PI, scalar2=-2 * PI, op0=ALU.is_gt,
                                                           op1=ALU.mult), r=[argB], w=[argB])
                    kb.op("dve", lambda e: e.tensor_tensor(out=a, in0=a, in1=m, op=ALU.add), r=[argB], w=[argB])
                    kb.op("dve", lambda e: e.tensor_scalar(out=m, in0=a, scalar1=-PI, scalar2=2 * PI, op0=ALU.is_lt,
                                                           op1=ALU.mult), r=[argB], w=[argB])
                    kb.op("dve", lambda e: e.tensor_tensor(out=a, in0=a, in1=m, op=ALU.add), r=[argB], w=[argB])
                    kb.op("act", lambda e: e.activation(out=dst[0:64, blk * 512:(blk + 1) * 512], in_=a, func=AF.Sin),
                          r=[argB], w=[dstB])

            sin_layer(w1, 33, zT, zB, h1, h1B, 33, 36, B["tmpA"][0])
            sin_layer(w2, 64, h1, h1B, h2, h2B, 35, 37, B["tmpA"][1])
            for n in range(NT):
                for cbk in range(4):
                    d_, o_ = cbk // 2, cbk % 2
                    b = nextbank()
                    kb.op("pe", lambda e: e.matmul(bank(b), lhsT=h2[0:64, n * 128:(n + 1) * 128],
                                                   rhs=w3[0:64, cbk * 512:(cbk + 1) * 512], start=True, stop=True),
                          r=[h2B, w3B], w=[PB[b]])
                    ti = cbk % 2
                    kb.op("act", lambda e: e.activation(out=tmpB[ti][:], in_=dec[:, cbk * 512:(cbk + 1) * 512],
                                                        func=AF.Exp, scale=negt01[:, n:n + 1]),
                          r=[decB, B["misc"]], w=[B["tmpB"][ti]])
                    kb.op("dve", lambda e: e.tensor_tensor(out=hbuf[:, d_ * 16 + n, o_ * 512:(o_ + 1) * 512],
                                                           in0=bank(b), in1=tmpB[ti][:], op=ALU.mult),
                          r=[PB[b], B["tmpB"][ti], RB], w=[hbB])
            HsBs = bufs(2)
            blocks = [(i, ri) for i in range(16) for ri in range(2)]
            loaded = {}

            def issue(bi):
                if bi < len(blocks):
                    i_, ri_ = blocks[bi]
                    fi_ = next_wb()
                    v_ = wb[fi_][:, :].rearrange("p (r f) -> p r f", r=32)
                    kb.dma("sp", rg_c, v_, Cd["fh"][i_ * 2 + ri_].rearrange("p (r f) -> p r f", r=32),
                           w=[B["wb"][fi_]])
                    loaded[bi] = (fi_, v_)

            issue(0)
            issue(1)
            for bi, (i, ri) in enumerate(blocks):
                fi, v = loaded.pop(bi)
                for o_ in range(2):
                    b = nextbank()
                    for rc in range(32):
                        kb.op("pe", lambda e: e.matmul(bank(b), lhsT=v[:, rc, :],
                                                       rhs=hbuf[:, rc, o_ * 512:(o_ + 1) * 512], start=(rc == 0),
                                                       stop=(rc == 31)),
                              r=[B["wb"][fi], hbB], w=[PB[b]], inc=(rc == 31))
                    dst = Hs[:, ri * 1024 + o_ * 512:ri * 1024 + (o_ + 1) * 512]
                    if ri == 0:
                        kb.op("dve", lambda e: e.tensor_tensor(out=dst, in0=bank(b),
                                                               in1=skb[:, o_ * 512:(o_ + 1) * 512], op=ALU.add),
                              r=[PB[b], skB], w=[HsBs[0]])
                    else:
                        kb.op("act", lambda e: e.copy(out=dst, in_=bank(b)), r=[PB[b]], w=[HsBs[1]])
                issue(bi + 2)
                kb.dma("sp", rg_st, Hd[j, i][:, ri * 1024:(ri + 1) * 1024], Hs[:, ri * 1024:(ri + 1) * 1024],
                       r=[HsBs[ri]], w=[HdB[j][i]])

        def mixer_even(s, layer, pre_norm=True, nxt=None, nst=None):
            j = layer // 2
            if pre_norm:
                norm_T(s, 0, layer)
            load_bcast(lnb, Wd["a_ln_g"][j], B["lnb"], 512)
            load_bcast(bsb, Wd["a_b_s"][j].rearrange("g p -> (g p)"), B["bsb"], 512)
            load_cols(Wd["b_conv_w"][j].rearrange("k (c p) -> (k c) p", p=128), 36, 0)
            kb.dma("pool", rg_w, wsN[:, :].rearrange("p (g q) -> p g q", g=4),
                   Wd["a_w_s"][j].rearrange("g p q -> p g q"), w=[B["wsN"]])
            b = nextbank()
            pT = bank(b).bitcast(BF16)[:, 0:512]
            for g in range(4):
                kb.op("pe", lambda e: e.transpose(out=pT[:, g * 128:(g + 1) * 128], in_=wsN[:, g * 128:(g + 1) * 128],
                                                  identity=ident[:]), r=[B["wsN"], B["ident"]], w=[PB[b]],
                      inc=(g == 3))
            kb.op("dve", lambda e: e.tensor_copy(out=wsT[:, :], in_=pT), r=[PB[b]], w=[B["wsT"]])
            wsv = wsT[:, :].rearrange("p (g q) -> p g q", g=4)
            bsv = bsb[:, :].rearrange("p (g q) -> p g q", g=4)
            win = Wd["ab_w_in"][j].rearrange("(k p) c -> p k c", p=128)
            yaT = Rv(0, 8192).rearrange("p (c t) -> p c t", c=4)
            zv = Rv(8192, 8192).rearrange("p (n c) -> p n c", n=16)
            x1 = Rv(16384, 8192).rearrange("p (n c) -> p n c", n=16)
            ybT = Rv(16384, 8192).rearrange("p (c t) -> p c t", c=4)
            x2T = Rv(24576, 8192).rearrange("p (c t) -> p c t", c=4)
            o = 32768
            raw = Rf(o, 2050); o += 4100
            acc = Rf(o, 2048); o += 4096
            pcb = Rv(o, 2048); o += 2048
            vg = Rf(o, 512); o += 1024
            vnb = Rv(o, 512); o += 512
            T1 = Rf(o, 512); o += 1024
            T2 = Rf(o, 512); o += 1024
            assert o <= 47104
            yaB, zvB, x1B, x2B, rawB, accB, pcbB, vgB, vnB, TB = bufs(10)
            kb.op("dve", lambda e: e.memset(raw[:, 0:1], 0.0), r=[RB], w=[rawB])
            kb.op("dve", lambda e: e.memset(raw[:, 2049:2050], 0.0), r=[RB], w=[rawB])
            zvT = bufs(16)
            x1T = bufs(16)
            ybB = Buf()
            w0, wt0 = get_wblock(("ein", layer, 0), win[:, :, 0:512])
            w1_, wt1 = get_wblock(("ein", layer, 1), win[:, :, 512:1024])
            for c in range(4):
                for tg in range(4):
                    b = nextbank()
                    for k in range(8):
                        kb.op("pe", lambda e: e.matmul(bank(b), lhsT=wt0[:, k, c * 128:(c + 1) * 128],
                                                       rhs=hT[:, k, 1 + tg * 512:1 + (tg + 1) * 512],
                                                       start=(k == 0), stop=(k == 7)),
                              r=[B["wb"][w0]] + B["regA"], w=[PB[b]], inc=(k == 7))
                    kb.op("act", lambda e: e.activation(out=yaT[:, c, tg * 512:(tg + 1) * 512], in_=bank(b),
                                                        func=GELU_FUNC), r=[PB[b], RB], w=[yaB])
            vgs = [vg, T1]
            vnbs = [vnb, T2.bitcast(BF16)[:, 0:512]]
            vgBs, vnBs = bufs(2), bufs(2)

            def a_stage1(n):
                v_, vn_, vB_, nB_ = vgs[n % 2], vnbs[n % 2], vgBs[n % 2], vnBs[n % 2]
                sc = 8 + (n % 2) * 4
                b = nextbank()
                for k in range(8):
                    kb.op("pe", lambda e: e.matmul(bank(b), lhsT=hT[:, k, 1 + n * 128:1 + (n + 1) * 128],
                                                   rhs=wt1[:, k, :], start=(k == 0), stop=(k == 7)),
                          r=[B["wb"][w1_], B["regA"][n]], w=[PB[b]], inc=(k == 7))
                kb.op("act", lambda e: e.activation(out=v_[:, :], in_=bank(b), func=GELU_FUNC), r=[PB[b], RB],
                      w=[vB_])
                kb.op("dve", lambda e: e.bn_stats(out=bn6[:, 0:6], in_=v_[:, :]), r=[vB_], w=[B["bn6"]])
                kb.op("dve", lambda e: e.bn_aggr(out=stat[:, sc:sc + 2], in_=bn6[:, 0:6]), r=[B["bn6"]],
                      w=[lnS[n % 2]])
                rstd_from_ss(stat[:, sc + 1:sc + 2], stat[:, sc + 2:sc + 3], stat[:, sc + 3:sc + 4], 1.0, 1e-5,
                             [lnS[n % 2]], [lnS[n % 2]])
                kb.op("dve", lambda e: e.tensor_scalar(out=v_[:, :], in0=v_[:, :], scalar1=stat[:, sc:sc + 1],
                                                       scalar2=stat[:, sc + 3:sc + 4], op0=ALU.subtract,
                                                       op1=ALU.mult), r=[lnS[n % 2]], w=[vB_])
                kb.op("dve", lambda e: e.tensor_tensor(out=vn_, in0=v_[:, :], in1=lnb[:, :], op=ALU.mult),
                      r=[vB_, B["lnb"]], w=[nB_])

            def a_stage2(n):
                vn_, nB_ = vnbs[n % 2], vnBs[n % 2]
                b2 = nextbank()
                sT = bank(b2).rearrange("p (g q) -> p g q", g=4)
                for g in range(4):
                    kb.op("pe", lambda e: e.matmul(sT[:, g, :], lhsT=vn_[:, g * 128:(g + 1) * 128], rhs=wsv[:, g, :],
                                                   start=True, stop=True), r=[nB_, B["wsT"]], w=[PB[b2]],
                          inc=(g == 3))
                ti = n % 2
                tv = tmpB[ti][:, :].rearrange("p (g q) -> p g q", g=4)
                kb.op("dve", lambda e: e.tensor_tensor(out=tv, in0=sT, in1=bsv, op=ALU.add),
                      r=[PB[b2], B["bsb"]], w=[B["tmpB"][ti]])
                kb.op("dve", lambda e: e.tensor_tensor(out=yaT[:, :, n * 128:(n + 1) * 128], in0=tv,
                                                       in1=yaT[:, :, n * 128:(n + 1) * 128], op=ALU.mult),
                      r=[B["tmpB"][ti]], w=[yaB])

            lnS = bufs(2)
            a_stage1(0)

            def a_step(n):
                if n + 1 < NT:
                    a_stage1(n + 1)
                a_stage2(n)
            a_steps = [(lambda n=n: a_step(n)) for n in range(NT)]
            pcbs = [pcb, pcb2[:, :]]
            pcbBs = [pcbB, Buf()]
            pend_t = []
            ci = 0
            for blk in range(3):
                if blk == 2:
                    while a_steps:
                        a_steps.pop(0)()
                wi = next_wb()
                wt = load_wblock(wi, win[:, :, 1024 + blk * 512:1024 + (blk + 1) * 512])
                for c in range(4):
                    cc = blk * 4 + c
                    for tg in range(4):
                        b = nextbank()
                        for k in range(8):
                            kb.op("pe", lambda e: e.matmul(bank(b), lhsT=wt[:, k, c * 128:(c + 1) * 128],
                                                           rhs=hT[:, k, 1 + tg * 512:1 + (tg + 1) * 512],
                                                           start=(k == 0), stop=(k == 7)),
                                  r=[B["wb"][wi]] + B["regA"], w=[PB[b]], inc=(k == 7))
                        kb.op("act", lambda e: e.copy(out=raw[:, 1 + tg * 512:1 + (tg + 1) * 512], in_=bank(b)),
                              r=[PB[b], RB], w=[rawB])
                    while pend_t:
                        pend_t.pop(0)()
                    kb.op("act", lambda e: e.activation(out=acc[:, :], in_=raw[:, 0:2048], func=AF.Copy,
                                                        scale=cols[:, cc:cc + 1]),
                          r=[rawB, B["cols"]], w=[accB])
                    kb.op("dve", lambda e: e.scalar_tensor_tensor(out=acc[:, :], in0=raw[:, 1:2049],
                                                                  scalar=cols[:, 12 + cc:13 + cc], in1=acc[:, :],
                                                                  op0=ALU.mult, op1=ALU.add),
                          r=[rawB, B["cols"]], w=[accB])
                    pc_ = pcbs[ci % 2]
                    pB_ = pcbBs[ci % 2]
                    ci += 1
                    dst = x2T[:, c, :] if blk == 2 else pc_
                    dB = x2B if blk == 2 else pB_
                    kb.op("dve", lambda e: e.scalar_tensor_tensor(out=dst, in0=raw[:, 2:2050],
                                                                  scalar=cols[:, 24 + cc:25 + cc], in1=acc[:, :],
                                                                  op0=ALU.mult, op1=ALU.add),
                          r=[rawB, accB, B["cols"]], w=[dB])
                    for _ in range(2):
                        if a_steps and blk < 2:
                            a_steps.pop(0)()
                    if blk < 2:
                        def tr(blk=blk, c=c, pc_=pc_, pB_=pB_):
                            tgt = zv if blk == 0 else x1
                            tB = zvT if blk == 0 else x1T
                            for h8 in range(2):
                                b = nextbank()
                                pT8 = bank(b).bitcast(BF16).rearrange("p (k t) -> p k t", k=8)
                                for q in range(8):
                                    n = h8 * 8 + q
                                    kb.op("pe", lambda e: e.transpose(out=pT8[:, q, :],
                                                                      in_=pc_[:, n * 128:(n + 1) * 128],
                                                                      identity=ident[:]), r=[pB_, B["ident"]],
                                          w=[PB[b]], inc=(q == 7))
                                kb.op("act", lambda e: e.copy(out=tgt[:, h8 * 8:(h8 + 1) * 8, c * 128:(c + 1) * 128],
                                                              in_=pT8), r=[PB[b]], w=tB[h8 * 8:(h8 + 1) * 8])
                        pend_t.append(tr)
            while pend_t:
                pend_t.pop(0)()
            def fwd_dft(order, srcT):
                for i in range(16):
                    fi = next_wb()
                    fv = wb[fi][:, :].rearrange("p (r s f) -> p r s f", r=2, s=16)
                    kb.dma("sp", rg_c, wb[fi][:, :], Cd["fz"][i], w=[B["wb"][fi]])
                    hi = i % 2
                    kb.dma("sp", rg_h, Hb[hi][:, :, :],
                           Hd[j, i].rearrange("p (r o c) -> p r o c", r=2, o=2)[:, :, order, :],
                           r=[HdB[j][i]], w=[B["Hb"][hi]])
                    bb = [nextbank(), nextbank()]
                    for ri in range(2):
                        for sc in range(16):
                            kb.op("pe", lambda e: e.matmul(bank(bb[ri]), lhsT=fv[:, ri, sc, :], rhs=zv[:, sc, :],
                                                           start=(sc == 0), stop=(sc == 15)),
                                  r=[B["wb"][fi], srcT[sc]], w=[PB[bb[ri]]], inc=(sc == 15))
                    zr, zi = bank(bb[0]), bank(bb[1])
                    hr, him = Hb[hi][:, 0, :], Hb[hi][:, 1, :]
                    rB = [PB[bb[0]], PB[bb[1]], B["Hb"][hi]]
                    kb.op("dve", lambda e: e.tensor_tensor(out=T1[:, :], in0=zr, in1=hr, op=ALU.mult), r=rB, w=[TB])
                    kb.op("dve", lambda e: e.tensor_tensor(out=T2[:, :], in0=zi, in1=him, op=ALU.mult), r=rB, w=[TB])
                    kb.op("dve", lambda e: e.tensor_tensor(out=Yv[:, i, :], in0=T1[:, :], in1=T2[:, :],
                                                           op=ALU.subtract), r=[TB], w=[B["Y"][i]])
                    kb.op("dve", lambda e: e.tensor_tensor(out=T1[:, :], in0=zr, in1=him, op=ALU.mult), r=rB, w=[TB])
                    kb.op("dve", lambda e: e.tensor_tensor(out=T2[:, :], in0=zi, in1=hr, op=ALU.mult), r=rB, w=[TB])
                    kb.op("dve", lambda e: e.tensor_tensor(out=Yv[:, 16 + i, :], in0=T1[:, :], in1=T2[:, :],
                                                           op=ALU.add), r=[TB], w=[B["Y"][16 + i]])

            def inv_dft(order):
                for tg in range(4):
                    bb = [nextbank() for _ in range(4)]
                    for fcg in range(4):
                        gi_ = next_wb()
                        gv = wb[gi_][:, :].rearrange("p (f t) -> p f t", f=8)
                        kb.dma("sp", rg_c, wb[gi_][:, :], Cd["ginv"][tg * 4 + fcg], w=[B["wb"][gi_]])
                        for fc in range(8):
                            fa = fcg * 8 + fc
                            for q in range(4):
                                st_, sp_ = (fa == 0), (fa == 31)
                                if order == 0:
                                    kb.op("pe", lambda e: e.matmul(bank(bb[q]), lhsT=gv[:, fc, q * 128:(q + 1) * 128],
                                                                   rhs=Yv[:, fa, :], start=st_, stop=sp_),
                                          r=[B["wb"][gi_], B["Y"][fa]], w=[PB[bb[q]]],
                                          inc=(sp_ or (fc == 7 and q == 3)))
                                else:
                                    kb.op("pe", lambda e: e.matmul(bank(bb[q]), lhsT=Yv[:, fa, q * 128:(q + 1) * 128],
                                                                   rhs=gv[:, fc, :], start=st_, stop=sp_),
                                          r=[B["wb"][gi_], B["Y"][fa]], w=[PB[bb[q]]],
                                          inc=(sp_ or (fc == 7 and q == 3)))
                    for q in range(4):
                        if order == 0:
                            n = tg * 4 + q
                            kb.op("dve", lambda e: e.tensor_tensor(out=zv[:, n, :], in0=bank(bb[q]), in1=x1[:, n, :],
                                                                   op=ALU.mult), r=[PB[bb[q]], x1T[n]], w=[zvT[n]])
                        else:
                            kb.op("dve", lambda e: e.tensor_tensor(out=ybT[:, q, tg * 512:(tg + 1) * 512],
                                                                   in0=bank(bb[q]),
                                                                   in1=x2T[:, q, tg * 512:(tg + 1) * 512],
                                                                   op=ALU.mult), r=[PB[bb[q]], x2B] + x1T, w=[ybB])

            fwd_dft(0, zvT)
            inv_dft(0)
            fwd_dft(1, zvT)
            inv_dft(1)
            out_proj(s, Wd["ab_w_out"][j], yaT, ybT, yaB, ybB, layer, nxt, nst)

        for st_ in plan:
            if st_[0] == "filt":
                filters(st_[1])
                kb.barrier()
        stages = [st_ for st_ in plan if st_[0] in ("even", "odd", "ffn")]
        xseq = {"done": False}
        for s in range(nseq):
            for k, st_ in enumerate(stages):
                nxt = None
                nst = None
                if k + 1 < len(stages):
                    nk = stages[k + 1]
                    nxt = (nk[1], 2 if nk[0] == "ffn" else 0)
                    nst = nk
                nsrc = None
                if k + 1 == len(stages) and s + 1 < nseq:
                    nst = stages[0]
                    if st_[0] == "ffn":
                        nxt = (nst[1], 2 if nst[0] == "ffn" else 0)
                        nsrc = s + 1
                pre_n = (k == 0) and not xseq["done"]
                xseq["done"] = nsrc is not None
                if st_[0] == "even":
                    mixer_even(s, st_[1], pre_n, nxt, nst)
                elif st_[0] == "odd":
                    mixer_odd(s, st_[1], pre_n, nxt, nst)
                elif st_[0] == "ffn":
                    ffn(s, st_[1], pre_n, nxt, nst, nsrc)
                kb.barrier()
        assert not pre, pre
        kb.finish()
    return nc


_NC_CACHE = {}


def run(inputs, nseq=SEQ_PER_CORE, plan=None, ncores=NCORES):
    key = (nseq, str(plan))
    if key not in _NC_CACHE:
        _NC_CACHE[key] = build(nseq, plan)
    nc = _NC_CACHE[key]
    C = _consts()
    x = np.ascontiguousarray(inputs["x"], dtype=np.float32)
    in_maps = []
    for c in range(ncores):
        m = {"x": x[c * nseq:(c + 1) * nseq]}
        for k in W_SPECS:
            m[k] = np.ascontiguousarray(inputs[k], dtype=np.float32)
        for k in CONST_SPECS:
            m[k] = C[k]
        in_maps.append(m)
    res = run_bass_kernel_spmd(nc, in_maps, core_ids=list(range(ncores)))
    return np.concatenate([res.results[c]["y"] for c in range(ncores)], axis=0)


def kernel(**inputs):
    return run(inputs).astype(np.float32)
```

```python
import math
import numpy as np
import ml_dtypes
from contextlib import ExitStack
import concourse.bass as bass
import concourse.mybir as mybir
from concourse.bass_utils import run_bass_kernel_spmd

F32, BF16 = mybir.dt.float32, mybir.dt.bfloat16
AF = mybir.ActivationFunctionType
ALU = mybir.AluOpType

L = 2048
D = 1024
NT = 16
DFF = 2816
NJ = 22
NCORES = 8
SEQ_PER_CORE = 4
NFFT = 4096
PI = math.pi

GELU_FUNC = AF.Gelu_apprx_tanh


class Buf:
    __slots__ = ("w", "r")

    def __init__(self):
        self.w = None
        self.r = {}


def bufs(n):
    return [Buf() for _ in range(n)]


class Ring:
    def __init__(self, kb, name, n):
        self.slots = [[kb.newsem(name), 0] for _ in range(n)]
        self.i = 0


class KB:
    EPOCH = 30000

    def __init__(self, nc, st):
        self.nc, self.st = nc, st
        self.eng = dict(pe=nc.tensor, act=nc.scalar, dve=nc.vector, pool=nc.gpsimd, sp=nc.sync)
        self.nsem = 0
        self.own = {e: set() for e in self.eng}
        self.cur = {}
        for e in self.eng:
            s = self.newsem(e)
            self.own[e].add(s)
            self.cur[e] = [s, 0]
        self.seen = {e: {} for e in self.eng}
        self.pend = {e: [] for e in self.eng}
        self.rings = []

    def newsem(self, name):
        self.nsem += 1
        return self.st.enter_context(self.nc.semaphore(f"{name}_{self.nsem}"))

    def ring(self, name, n):
        r = Ring(self, name, n)
        self.rings.append(r)
        return r

    @staticmethod
    def _deps(r, w):
        d = {}
        for b in r:
            if b.w is not None and d.get(b.w[0], 0) < b.w[1]:
                d[b.w[0]] = b.w[1]
        for b in w:
            if b.w is not None and d.get(b.w[0], 0) < b.w[1]:
                d[b.w[0]] = b.w[1]
            for s, v in b.r.items():
                if d.get(s, 0) < v:
                    d[s] = v
        return d

    def _wait(self, e, deps, skip_own):
        eng = self.eng[e]
        seen = self.seen[e]
        for s, v in deps.items():
            if seen.get(s, 0) >= v:
                continue
            if skip_own and s in self.own[e]:
                continue
            eng.wait_ge(s, v)
            seen[s] = v

    def op(self, e, fn, r=(), w=(), inc=True):
        self._wait(e, self._deps(r, w), skip_own=(e == "pe"))
        ins = fn(self.eng[e])
        if inc:
            c = self.cur[e]
            if c[1] >= self.EPOCH:
                c[0] = self.newsem(e)
                self.own[e].add(c[0])
                c[1] = 0
            ins.then_inc(c[0], 1)
            c[1] += 1
            s, v = c[0], c[1]
            for rr, ww in self.pend[e] + [(r, w)]:
                for b in rr:
                    if b.r.get(s, 0) < v:
                        b.r[s] = v
                for b in ww:
                    b.w = (s, v)
                    b.r = {}
            self.pend[e] = []
        else:
            self.pend[e].append((r, w))
        return ins

    def dma(self, q, ring, out, in_, r=(), w=()):
        slot = ring.slots[ring.i]
        ring.i = (ring.i + 1) % len(ring.slots)
        deps = self._deps(r, w)
        if slot[1] > 0 and deps.get(slot[0], 0) < slot[1]:
            deps[slot[0]] = slot[1]
        self._wait(q, deps, False)
        self.eng[q].dma_start(out=out, in_=in_).then_inc(slot[0], 16)
        slot[1] += 16
        s, v = slot[0], slot[1]
        for b in r:
            if b.r.get(s, 0) < v:
                b.r[s] = v
        for b in w:
            b.w = (s, v)
            b.r = {}

    def barrier(self):
        ids = {}
        for e in self.eng:
            assert not self.pend[e]
            c = self.cur[e]
            if c[1] > 0:
                ids[c[0]] = c[1]
        for rg in self.rings:
            for s, v in rg.slots:
                if v > 0:
                    ids[s] = v
        for e in self.eng:
            if e in ("pe", "sp"):
                continue
            self._wait(e, dict(ids), skip_own=True)

    def finish(self):
        for e in self.eng:
            assert not self.pend[e]
        for rg in self.rings:
            for s, v in rg.slots:
                if v > 0:
                    self.nc.sync.wait_ge(s, v)


_CONST = {}


def _consts():
    if _CONST:
        return _CONST
    bf = ml_dtypes.bfloat16
    N = NFFT
    f = np.arange(L, dtype=np.float64)
    om = 2 * np.pi * (f + 0.5) / N
    s = np.arange(L, dtype=np.float64)
    ang = np.outer(s, om)
    FzT = np.stack([np.cos(ang), -np.sin(ang)], 0)
    fz = FzT.reshape(2, 16, 128, 16, 128).transpose(3, 2, 0, 1, 4)
    _CONST["fz"] = np.ascontiguousarray(fz).reshape(16, 128, 4096).astype(bf)
    idxb = (N - 1 - s)
    angb = np.outer(idxb, om)
    mask = (s < L - 1).astype(np.float64)[:, None]
    FhT_re = np.concatenate([np.cos(ang), -np.cos(angb) * mask], 0)
    FhT_im = np.concatenate([-np.sin(ang), np.sin(angb) * mask], 0)
    FhT = np.stack([FhT_re, FhT_im], 0)
    fh = FhT.reshape(2, 32, 128, 16, 128).transpose(3, 0, 2, 1, 4)
    _CONST["fh"] = np.ascontiguousarray(fh).reshape(32, 128, 4096).astype(bf)
    del FhT, FhT_re, FhT_im, fh
    t = np.arange(L, dtype=np.float64)
    angt = np.outer(om, t)
    GT = np.concatenate([np.cos(angt), -np.sin(angt)], 0) * (2.0 / N)
    g = GT.reshape(4, 8, 128, 4, 512).transpose(3, 0, 2, 1, 4)
    _CONST["ginv"] = np.ascontiguousarray(g).reshape(16, 128, 4096).astype(bf)
    tt = np.arange(L, dtype=np.float32)
    t01 = tt / np.float32(L - 1)
    w = (np.float32(2 * math.pi) * tt / np.float32(L)).astype(np.float32)
    bands = np.linspace(1e-4, 15, 16, dtype=np.float32)
    fw = w[:, None] * bands[None, :]
    z = np.concatenate([t01[:, None], np.cos(fw), -np.sin(fw)], -1).astype(np.float32)
    _CONST["zT"] = np.ascontiguousarray(z.T)
    _CONST["negt01"] = np.ascontiguousarray((-t01).reshape(16, 128).T)
    ic = np.zeros((4, 16), np.float32)
    for gi, wv in enumerate((2, 4, 8, 16)):
        left = wv // 2
        right = wv - left - 1
        for k in range(8):
            for pos, tpos in ((k, k), (8 + k, L - 8 + k)):
                cnt = min(tpos + right, L - 1) - max(tpos - left, 0) + 1
                ic[gi, pos] = 1.0 / cnt
    _CONST["icnt"] = np.ascontiguousarray(np.broadcast_to(ic.reshape(1, 64), (128, 64)))
    _CONST["ident"] = np.eye(128, dtype=np.float32).astype(bf)
    _CONST["identf"] = np.eye(128, dtype=np.float32)
    return _CONST


CONST_SPECS = dict(fz=([16, 128, 4096], BF16), fh=([32, 128, 4096], BF16), ginv=([16, 128, 4096], BF16),
                   zT=([33, 2048], F32), negt01=([128, 16], F32), icnt=([128, 64], F32),
                   ident=([128, 128], BF16), identf=([128, 128], F32))

W_SPECS = dict(
    norm_g=[4, 4, 1024], ffn_w_gu=[4, 1024, 5632], ffn_w_down=[4, 2816, 1024],
    ab_w_in=[2, 1024, 2560], ab_w_out=[2, 1024, 1024], a_ln_g=[2, 512], a_w_s=[2, 4, 128, 128],
    a_b_s=[2, 4, 128], b_conv_w=[2, 3, 1536], b_filt_w1=[2, 33, 64], b_filt_b1=[2, 64],
    b_filt_freq=[2, 2, 64], b_filt_w2=[2, 64, 64], b_filt_b2=[2, 64], b_filt_w3=[2, 64, 2048],
    b_decay=[2, 2048], b_skip=[2, 2, 512], cd_w_in=[2, 1024, 2048], cd_w_out=[2, 1024, 1024],
    c_w=[2, 4, 128, 128], c_scale=[2, 512], d_conv_w=[2, 3, 512])


def build(nseq=SEQ_PER_CORE, plan=None):
    if plan is None:
        plan = [("filt", 0), ("filt", 1)]
        for i in range(4):
            plan += [("even" if i % 2 == 0 else "odd", i), ("ffn", i)]
    nc = bass.Bass("TRN2", target_bir_lowering=False)
    xin = nc.dram_tensor("x", [nseq, L, D], F32, kind="ExternalInput").ap()
    yout = nc.dram_tensor("y", [nseq, L, D], F32, kind="ExternalOutput").ap()
    Wd = {k: nc.dram_tensor(k, s, F32, kind="ExternalInput").ap() for k, s in W_SPECS.items()}
    Cd = {k: nc.dram_tensor(k, s, dt, kind="ExternalInput").ap() for k, (s, dt) in CONST_SPECS.items()}
    Hd = nc.dram_tensor("Hscr", [2, 16, 128, 2048], F32).ap()
    HdB = [bufs(16) for _ in range(2)]

    with ExitStack() as st:
        kb = KB(nc, st)
        sb = lambda n, s, d: st.enter_context(nc.sbuf_tensor("s_" + n, s, d))
        ident = sb("ident", [128, 128], BF16)
        identf = sb("identf", [128, 128], F32)
        regA = sb("regA", [128, 16400], BF16)
        R = sb("R", [128, 47104], BF16)
        wb = [sb(f"wb{i}", [128, 4096], BF16) for i in range(3)]
        xt = [sb(f"xt{i}", [128, 1024], F32) for i in range(3)]
        xs = [sb(f"xs{i}", [128, 1024], BF16) for i in range(3)]
        junk = sb("junk", [128, 1024], BF16)
        Hb = [sb(f"Hb{i}", [128, 2, 512], F32) for i in range(2)]
        gb = [sb(f"gb{i}", [128, 1024], F32) for i in range(2)]
        tmpA = [sb(f"tmpA{i}", [128, 512], F32) for i in range(2)]
        tmpB = [sb(f"tmpB{i}", [128, 512], F32) for i in range(2)]
        stat = sb("stat", [128, 64], F32)
        bn6 = sb("bn6", [128, 8], F32)
        cols = sb("cols", [128, 64], F32)
        rows = sb("rows", [64, 128], F32)
        lnb = sb("lnb", [128, 512], F32)
        bsb = sb("bsb", [128, 512], F32)
        wsT = sb("wsT", [128, 512], BF16)
        wsN = sb("wsN", [128, 512], BF16)
        cwg = sb("cwg", [128, 512], BF16)
        pcb2 = sb("pcb2", [128, 2048], BF16)
        icnt = sb("icnt", [128, 64], F32)
        negt01 = sb("negt01", [128, 16], F32)
        B = dict(ident=Buf(), regA=bufs(16), Y=bufs(32), wb=bufs(3), xt=bufs(3), xs=bufs(3), Hb=bufs(2),
                 gb=bufs(2), tmpA=bufs(2), tmpB=bufs(2), stat=Buf(), bn6=Buf(), cols=Buf(), rows=Buf(),
                 lnb=Buf(), bsb=Buf(), wsT=Buf(), wsN=Buf(), cwg=Buf(), misc=Buf())
        PS = [st.enter_context(nc.psum_tensor(f"ps{i}", [128, 1024], F32)) for i in range(4)]
        PB = bufs(8)
        statR = bufs(4)
        pnS = bufs(4)
        pstate = {"b": 0}

        def bank(i):
            return PS[i // 2][:, (i % 2) * 512:(i % 2) * 512 + 512]

        def nextbank():
            i = pstate["b"]
            pstate["b"] = (i + 1) % 8
            return i

        def nextpair():
            i = pstate["b"]
            if i % 2:
                i = (i + 1) % 8
            pstate["b"] = (i + 2) % 8
            return i // 2

        rg_x = kb.ring("x", 4)
        rg_st = kb.ring("st", 4)
        rg_w = kb.ring("w", 4)
        rg_c = kb.ring("c", 4)
        rg_h = kb.ring("h", 2)
        rg_m = kb.ring("m", 4)
        rg_pst = kb.ring("pst", 4)

        YB = [[bufs(NT) for _ in range(nseq)]][0]

        hT = regA[:, :].rearrange("p (k t) -> p k t", k=8)
        Yv = regA[:, 0:16384].rearrange("p (f c) -> p f c", c=512)

        def Rv(off, n):
            return R[:, off:off + n]

        def Rf(off, n):
            return R[:, off:off + 2 * n].bitcast(F32)

        kb.dma("sp", rg_m, ident[:], Cd["ident"], w=[B["ident"]])
        kb.dma("sp", rg_m, identf[:], Cd["identf"], w=[B["ident"]])
        kb.dma("sp", rg_m, icnt[:], Cd["icnt"], w=[B["misc"]])
        kb.dma("sp", rg_m, negt01[:], Cd["negt01"], w=[B["misc"]])
        kb.op("dve", lambda e: e.memset(regA[:, :], 0.0), w=B["regA"] + B["Y"])
        RB = Buf()
        kb.op("pool", lambda e: e.memset(R[:, :], 0.0), w=[RB])
        for s in range(nseq):
            for h in range(4):
                kb.dma("sp", rg_st, yout[s, h * 512:(h + 1) * 512, :], xin[s, h * 512:(h + 1) * 512, :],
                       w=YB[s][h * 4:(h + 1) * 4])

        def load_bcast(dst, src_row, buf, n):
            kb.dma("sp", rg_m, dst[:, 0:n], src_row.partition_broadcast(128), w=[buf])

        def load_cols(src2d, nrows, col0):
            kb.dma("sp", rg_m, rows[0:nrows, :], src2d, w=[B["rows"]])
            b = nextbank()
            kb.op("pe", lambda e: e.transpose(out=bank(b)[:, 0:nrows], in_=rows[0:nrows, :],
                                              identity=identf[0:nrows, 0:nrows]),
                  r=[B["rows"], B["ident"]], w=[PB[b]])
            kb.op("dve", lambda e: e.tensor_copy(out=cols[:, col0:col0 + nrows], in_=bank(b)[:, 0:nrows]),
                  r=[PB[b]], w=[B["cols"]])

        def rstd_from_ss(ss_ap, std_ap, rstd_ap, scale, eps, rbufs, wbufs):
            kb.op("act", lambda e: e.activation(out=std_ap, in_=ss_ap, func=AF.Sqrt, scale=scale, bias=eps),
                  r=rbufs, w=wbufs)
            kb.op("dve", lambda e: e.reciprocal(out=rstd_ap, in_=std_ap), r=wbufs, w=wbufs)

        def norm_T(s, gidx, layer):
            gi = gidx % 2
            load_bcast(gb[gi], Wd["norm_g"][layer, gidx, :], B["gb"][gi], 1024)
            kb.op("dve", lambda e: e.memset(hT[:, :, 0:1], 0.0), w=B["regA"] + B["Y"])
            kb.op("dve", lambda e: e.memset(hT[:, :, 2049:2050], 0.0), w=B["regA"] + B["Y"])
            for n in range(NT):
                xi = n % 3
                sc0 = 32 + (n % 4) * 4
                sB = [statR[n % 4]]
                kb.dma("sp", rg_x, xt[xi][:], yout[s, n * 128:(n + 1) * 128, :], r=[YB[s][n]], w=[B["xt"][xi]])
                kb.op("act", lambda e: e.activation(out=junk[:], in_=xt[xi][:], func=AF.Square,
                                                    accum_out=stat[:, sc0:sc0 + 1]), r=[B["xt"][xi]], w=sB)
                rstd_from_ss(stat[:, sc0:sc0 + 1], stat[:, sc0 + 1:sc0 + 2], stat[:, sc0 + 2:sc0 + 3], 1.0 / D, 1e-6,
                             sB, sB)
                norm_tail(xi, n, gi, sc0, sB)

        tails = []

        def norm_tail(xi, n, gi, sc0, sB, defer=False):
            si = n % 3
            kb.op("dve", lambda e: e.scalar_tensor_tensor(out=xs[si][:], in0=xt[xi][:],
                                                          scalar=stat[:, sc0 + 2:sc0 + 3],
                                                          in1=gb[gi][:], op0=ALU.mult, op1=ALU.mult),
                  r=[B["xt"][xi], B["gb"][gi]] + sB, w=[B["xs"][si]])

            def tail():
                b = nextbank()
                pT = bank(b).bitcast(BF16).rearrange("p (k t) -> p k t", k=8)
                for k in range(8):
                    kb.op("pe", lambda e: e.transpose(out=pT[:, k, :], in_=xs[si][:, k * 128:(k + 1) * 128],
                                                      identity=ident[:]),
                          r=[B["xs"][si], B["ident"]], w=[PB[b]], inc=(k == 7))
                kb.op("act", lambda e: e.copy(out=hT[:, :, 1 + n * 128:1 + (n + 1) * 128], in_=pT),
                      r=[PB[b]], w=[B["regA"][n]])
            if defer:
                tails.append(tail)
            else:
                tail()

        def flush_tails(keep=0):
            while len(tails) > keep:
                tails.pop(0)()

        pn_state = {"i": 0}
        mids = []

        def flush_mids():
            while mids:
                mids.pop(0)()

        def postnorm(pp, s, n, gi, fuse_next=False, nsrc=None):
            P2 = PS[pp][:, :]
            xi = pn_state["i"] % 3
            sl = pn_state["i"] % 4
            pn_state["i"] += 1
            sc = 16 + sl * 4
            sB = [pnS[sl]]
            ta = [tmpA[0], tmpA[1]]
            kb.dma("sp", rg_x, xt[xi][:], yout[s, n * 128:(n + 1) * 128, :], r=[YB[s][n]], w=[B["xt"][xi]])
            kb.op("act", lambda e: e.activation(out=junk[:], in_=P2, func=AF.Square, accum_out=stat[:, sc:sc + 1]),
                  r=[PB[2 * pp], PB[2 * pp + 1]], w=sB)
            rstd_from_ss(stat[:, sc:sc + 1], stat[:, sc + 1:sc + 2], stat[:, sc + 2:sc + 3], 1.0 / D, 1e-6, sB, sB)
            for hf in range(2):
                kb.op("dve", lambda e: e.tensor_tensor(out=ta[hf][:], in0=P2[:, hf * 512:(hf + 1) * 512],
                                                       in1=gb[gi][:, hf * 512:(hf + 1) * 512], op=ALU.mult),
                      r=[PB[2 * pp + hf], B["gb"][gi]], w=[B["tmpA"][hf]])
            for hf in range(2):
                kb.op("dve", lambda e: e.scalar_tensor_tensor(out=xt[xi][:, hf * 512:(hf + 1) * 512], in0=ta[hf][:],
                                                              scalar=stat[:, sc + 2:sc + 3],
                                                              in1=xt[xi][:, hf * 512:(hf + 1) * 512],
                                                              op0=ALU.mult, op1=ALU.add),
                      r=[B["tmpA"][hf]] + sB, w=[B["xt"][xi]])
            kb.dma("pool", rg_pst, yout[s, n * 128:(n + 1) * 128, :], xt[xi][:], r=[B["xt"][xi]], w=[YB[s][n]])
            flush_mids()
            if fuse_next:
                def mid(xi=xi, n=n):
                    sc0 = 32 + (n % 4) * 4
                    nB = [statR[n % 4]]
                    if nsrc is not None:
                        xi = pn_state["i"] % 3
                        pn_state["i"] += 1
                        kb.dma("sp", rg_x, xt[xi][:], yout[nsrc, n * 128:(n + 1) * 128, :], r=[YB[nsrc][n]],
                               w=[B["xt"][xi]])
                    kb.op("act", lambda e: e.activation(out=junk[:], in_=xt[xi][:], func=AF.Square,
                                                        accum_out=stat[:, sc0:sc0 + 1]), r=[B["xt"][xi]], w=nB)
                    rstd_from_ss(stat[:, sc0:sc0 + 1], stat[:, sc0 + 1:sc0 + 2], stat[:, sc0 + 2:sc0 + 3], 1.0 / D,
                                 1e-6, nB, nB)
                    norm_tail(xi, n, 0, sc0, nB, defer=True)
                mids.append(mid)

        def prep_next(nxt):
            if nxt is not None:
                load_bcast(gb[0], Wd["norm_g"][nxt[0], nxt[1], :], B["gb"][0], 1024)

        def load_wblock(i, src):
            shp = src.shape
            dst = wb[i][:, 0:shp[1] * shp[2]].rearrange("p (k c) -> p k c", k=shp[1])
            kb.dma("pool", rg_w, dst, src, w=[B["wb"][i]])
            return dst

        wstate = {"i": 0}

        def next_wb():
            i = wstate["i"]
            wstate["i"] = (i + 1) % 3
            return i

        pre = {}

        def get_wblock(key, src):
            if key in pre:
                return pre.pop(key)
            wi_ = next_wb()
            return wi_, load_wblock(wi_, src)

        def get_wgu(layer, jb, key):
            if key in pre:
                return pre.pop(key)
            wgu_ = Wd["ffn_w_gu"][layer].rearrange("(k p) c -> p k c", p=128)
            wi_ = next_wb()
            wv_ = wb[wi_][:, :].rearrange("p (k g c) -> p k g c", k=8, g=2)
            kb.dma("pool", rg_w, wv_[:, :, 0, :], wgu_[:, :, jb * 256:(jb + 1) * 256], w=[B["wb"][wi_]])
            kb.dma("pool", rg_w, wv_[:, :, 1, :], wgu_[:, :, DFF + jb * 256:DFF + (jb + 1) * 256],
                   w=[B["wb"][wi_]])
            return wi_, wv_

        def prefetch(nst, idx):
            if nst is None:
                return
            kind, layer = nst
            if kind == "ffn":
                key = ("wgu", layer, 0, idx)
                pre[key] = get_wgu(layer, idx, key)
            elif kind == "even":
                win_ = Wd["ab_w_in"][layer // 2].rearrange("(k p) c -> p k c", p=128)
                key = ("ein", layer, idx)
                pre[key] = get_wblock(key, win_[:, :, idx * 512:(idx + 1) * 512])
            elif kind == "odd" and idx == 0:
                win_ = Wd["cd_w_in"][layer // 2].rearrange("(k p) c -> p k c", p=128)
                key = ("oin", layer, 0)
                pre[key] = get_wblock(key, win_[:, :, 0:512])

        def out_proj(s, wout, aT, bT, aB, bBufs, layer, nxt=None, nst=None):
            load_bcast(gb[1], Wd["norm_g"][layer, 1, :], B["gb"][1], 1024)
            prep_next(nxt)
            if nxt is not None:
                kb.op("dve", lambda e: e.memset(hT[:, :, 0:1], 0.0), w=B["regA"] + B["Y"])
                kb.op("dve", lambda e: e.memset(hT[:, :, 2049:2050], 0.0), w=B["regA"] + B["Y"])
            wv = wout.rearrange("(k p) c -> p k c", p=128)
            wi = [next_wb(), next_wb()]
            wt = [load_wblock(wi[c], wv[:, :, c * 512:(c + 1) * 512]) for c in range(2)]
            prefetch(nst, 0)
            for n in range(NT):
                pp = nextpair()
                for c in range(2):
                    for k in range(8):
                        src = aT if k < 4 else bT
                        kb.op("pe", lambda e: e.matmul(bank(2 * pp + c), lhsT=src[:, k % 4, n * 128:(n + 1) * 128],
                                                       rhs=wt[c][:, k, :], start=(k == 0), stop=(k == 7)),
                              r=[aB, bBufs, B["wb"][wi[c]]], w=[PB[2 * pp + c]], inc=(k == 7))
                flush_tails(keep=0)
                postnorm(pp, s, n, 1, fuse_next=(nxt is not None))
            flush_mids()
            flush_tails()
            prefetch(nst, 1)

        def ffn(s, layer, pre_norm=True, nxt=None, nst=None, nsrc=None):
            if pre_norm:
                norm_T(s, 2, layer)
            load_bcast(gb[1], Wd["norm_g"][layer, 3, :], B["gb"][1], 1024)
            wgu = Wd["ffn_w_gu"][layer].rearrange("(k p) c -> p k c", p=128)
            wdn = Wd["ffn_w_down"][layer].rearrange("(j p) c -> p j c", p=128)
            hff = Rv(0, 22528).rearrange("p (j t) -> p j t", j=NJ)
            wd = Rv(22528, 22528).rearrange("p (j c) -> p j c", j=NJ)
            hffB, wdB = Buf(), Buf()
            for half in range(2):
                for jb in range(NJ // 2):
                    wi, wv = get_wgu(layer, jb, ("wgu", layer, half, jb))
                    if half == 0:
                        kb.dma("pool", rg_w, wd[:, 2 * jb:2 * jb + 2, :], wdn[:, 2 * jb:2 * jb + 2, :], r=[RB],
                               w=[wdB])
                    for jj in range(2):
                        j = jb * 2 + jj
                        for tg in range(2):
                            t0 = 1 + half * 1024 + tg * 512
                            bg, bu = nextbank(), nextbank()
                            for g_, bb in ((0, bg), (1, bu)):
                                for k in range(8):
                                    kb.op("pe", lambda e: e.matmul(bank(bb), lhsT=wv[:, k, g_, jj * 128:(jj + 1) * 128],
                                                                   rhs=hT[:, k, t0:t0 + 512], start=(k == 0),
                                                                   stop=(k == 7)),
                                          r=[B["wb"][wi]] + B["regA"][half * 8:half * 8 + 8], w=[PB[bb]],
                                          inc=(k == 7))
                            ti = (j * 2 + tg) % 2
                            kb.op("act", lambda e: e.activation(out=tmpB[ti][:], in_=bank(bg), func=AF.Silu),
                                  r=[PB[bg]], w=[B["tmpB"][ti]])
                            kb.op("dve", lambda e: e.tensor_tensor(out=hff[:, j, tg * 512:(tg + 1) * 512],
                                                                   in0=bank(bu), in1=tmpB[ti][:], op=ALU.mult),
                                  r=[PB[bu], B["tmpB"][ti], RB], w=[hffB])
                if half == 0:
                    prep_next(nxt)
                    for jb_ in range(2):
                        key_ = ("wgu", layer, 1, jb_)
                        pre[key_] = get_wgu(layer, jb_, key_)
                else:
                    prefetch(nst, 0)
                    prefetch(nst, 1)
                for nn in range(8):
                    n = half * 8 + nn
                    pp = nextpair()
                    for c in range(2):
                        for j in range(NJ):
                            kb.op("pe", lambda e: e.matmul(bank(2 * pp + c), lhsT=hff[:, j, nn * 128:(nn + 1) * 128],
                                                           rhs=wd[:, j, c * 512:(c + 1) * 512], start=(j == 0),
                                                           stop=(j == NJ - 1)),
                                  r=[hffB, wdB], w=[PB[2 * pp + c]], inc=(j == NJ - 1))
                    flush_tails(keep=0)
                    postnorm(pp, s, n, 1, fuse_next=(nxt is not None), nsrc=nsrc)
                flush_mids()
                flush_tails()

        def mixer_odd(s, layer, pre_norm=True, nxt=None, nst=None):
            j = layer // 2
            if pre_norm:
                norm_T(s, 0, layer)
            load_cols(Wd["c_scale"][j].rearrange("(r p) -> r p", p=128), 4, 0)
            load_cols(Wd["d_conv_w"][j].rearrange("k (c p) -> (k c) p", p=128), 12, 4)
            kb.dma("pool", rg_w, cwg[:, :].rearrange("p (g c) -> p g c", g=4),
                   Wd["c_w"][j].rearrange("g c d -> c g d"), w=[B["cwg"]])
            cwv = cwg[:, :].rearrange("p (g c) -> p g c", g=4)
            win = Wd["cd_w_in"][j].rearrange("(k p) c -> p k c", p=128)
            ycT = Rv(0, 8192).rearrange("p (c t) -> p c t", c=4)
            ydT = Rv(8192, 8192).rearrange("p (c t) -> p c t", c=4)
            o = 16384
            Pp = Rf(o, 2080); o += 4160
            Aa = Rf(o, 2080); o += 4160
            Ab = Rf(o, 2080); o += 4160
            pl = Rv(o, 2048); o += 2048
            ccs = Rf(o, 2048); o += 4096
            qq = Rf(o, 2050); o += 4100
            acc = Rf(o, 2048); o += 4096
            assert o <= 47104
            ycB, ydB, PpB, AB_, plB, ccB, qB, accB = bufs(8)
            kb.op("dve", lambda e: e.memset(Pp[:, 0:16], 0.0), r=[RB], w=[PpB])
            kb.op("dve", lambda e: e.memset(Pp[:, 2064:2080], 0.0), r=[RB], w=[PpB])
            kb.op("dve", lambda e: e.memset(qq[:, 0:1], 0.0), r=[RB], w=[qB])
            kb.op("dve", lambda e: e.memset(qq[:, 2049:2050], 0.0), r=[RB], w=[qB])
            w0, wt0 = get_wblock(("oin", layer, 0), win[:, :, 0:512])

            def c_proj(g):
                for tg in range(4):
                    b = nextbank()
                    for k in range(8):
                        kb.op("pe", lambda e: e.matmul(bank(b), lhsT=wt0[:, k, g * 128:(g + 1) * 128],
                                                       rhs=hT[:, k, 1 + tg * 512:1 + (tg + 1) * 512],
                                                       start=(k == 0), stop=(k == 7)),
                              r=[B["wb"][w0]] + B["regA"], w=[PB[b]], inc=(k == 7))
                    kb.op("act", lambda e: e.copy(out=Pp[:, 16 + tg * 512:16 + (tg + 1) * 512], in_=bank(b)),
                          r=[PB[b], RB], w=[PpB])

            def c_chain(g):
                wv_ = (2, 4, 8, 16)[g]
                left = wv_ // 2
                src, dst_list = Pp, [Aa, Ab]
                sh = 1
                li = 0
                while sh < wv_:
                    dst = dst_list[li % 2]
                    n_ = 2080 - sh
                    s_ = src
                    kb.op("dve", lambda e: e.tensor_tensor(out=dst[:, 0:n_], in0=s_[:, 0:n_], in1=s_[:, sh:sh + n_],
                                                            op=ALU.add), r=[PpB, AB_], w=[AB_])
                    src = dst
                    sh *= 2
                    li += 1
                a_ = src
                kb.op("dve", lambda e: e.scalar_tensor_tensor(out=pl[:, :], in0=a_[:, 16 - left:16 - left + 2048],
                                                              scalar=1.0 / wv_, in1=Pp[:, 16:16 + 2048],
                                                              op0=ALU.mult, op1=ALU.subtract),
                      r=[AB_, PpB], w=[plB])
                for (c0, t0) in ((0, 0), (8, L - 8)):
                    kb.op("dve", lambda e: e.tensor_tensor(out=bn6[:, 0:8],
                                                           in0=a_[:, 16 - left + t0:16 - left + t0 + 8],
                                                           in1=icnt[:, g * 16 + c0:g * 16 + c0 + 8], op=ALU.mult),
                          r=[AB_, B["misc"]], w=[B["bn6"]])
                    kb.op("dve", lambda e: e.tensor_tensor(out=pl[:, t0:t0 + 8], in0=bn6[:, 0:8],
                                                           in1=Pp[:, 16 + t0:16 + t0 + 8], op=ALU.subtract),
                          r=[B["bn6"], PpB], w=[plB])

            def c_mix(g):
                for tg in range(4):
                    b = nextbank()
                    kb.op("pe", lambda e: e.matmul(bank(b), lhsT=cwv[:, g, :], rhs=pl[:, tg * 512:(tg + 1) * 512],
                                                   start=True, stop=True), r=[B["cwg"], plB], w=[PB[b]])
                    kb.op("act", lambda e: e.activation(out=ycT[:, g, tg * 512:(tg + 1) * 512], in_=bank(b),
                                                        func=AF.Copy, scale=cols[:, g:g + 1]),
                          r=[PB[b], B["cols"], RB], w=[ycB])

            c_proj(0)
            for g in range(4):
                c_chain(g)
                if g + 1 < 4:
                    c_proj(g + 1)
                c_mix(g)
            wis = [next_wb(), next_wb(), next_wb()]
            wts = [load_wblock(wis[q], win[:, :, 512 * (q + 1):512 * (q + 2)]) for q in range(3)]
            for c in range(4):
                def proj(q, tg):
                    b = nextbank()
                    for k in range(8):
                        kb.op("pe", lambda e: e.matmul(bank(b), lhsT=wts[q][:, k, c * 128:(c + 1) * 128],
                                                       rhs=hT[:, k, 1 + tg * 512:1 + (tg + 1) * 512],
                                                       start=(k == 0), stop=(k == 7)),
                              r=[B["wb"][wis[q]]] + B["regA"], w=[PB[b]], inc=(k == 7))
                    return b
                for tg in range(4):
                    b = proj(1, tg)
                    kb.op("act", lambda e: e.copy(out=ccs[:, tg * 512:(tg + 1) * 512], in_=bank(b)),
                          r=[PB[b], RB], w=[ccB])
                for tg in range(4):
                    b = proj(2, tg)
                    kb.op("dve", lambda e: e.tensor_tensor(out=qq[:, 1 + tg * 512:1 + (tg + 1) * 512], in0=bank(b),
                                                           in1=ccs[:, tg * 512:(tg + 1) * 512], op=ALU.mult),
                          r=[PB[b], ccB, RB], w=[qB])
                kb.op("act", lambda e: e.activation(out=acc[:, :], in_=qq[:, 0:2048], func=AF.Copy,
                                                    scale=cols[:, 4 + c:5 + c]), r=[qB, B["cols"]], w=[accB])
                for kk in (1, 2):
                    kb.op("dve", lambda e: e.scalar_tensor_tensor(out=acc[:, :], in0=qq[:, kk:kk + 2048],
                                                                  scalar=cols[:, 4 + kk * 4 + c:5 + kk * 4 + c],
                                                                  in1=acc[:, :], op0=ALU.mult, op1=ALU.add),
                          r=[qB, B["cols"]], w=[accB])
                for tg in range(4):
                    b = proj(0, tg)
                    kb.op("dve", lambda e: e.tensor_tensor(out=ydT[:, c, tg * 512:(tg + 1) * 512], in0=bank(b),
                                                           in1=acc[:, tg * 512:(tg + 1) * 512], op=ALU.mult),
                          r=[PB[b], accB], w=[ydB])
            out_proj(s, Wd["cd_w_out"][j], ycT, ydT, ycB, ydB, layer, nxt, nst)

        def filters(j):
            hbuf = Rv(0, 32768).rearrange("p (r c) -> p r c", r=32)
            o = 32768
            zT = Rf(o, 2048); o += 4096
            h1 = Rf(o, 2048); o += 4096
            h2 = Rf(o, 2048); o += 4096
            arg = Rf(o, 512); o += 1024
            msk = Rf(o, 512); o += 1024
            assert o <= 47104
            hbB, zB, h1B, h2B, argB, w3B, decB, skB = bufs(8)
            w1 = tmpA[0]
            w2 = tmpA[1]
            w3 = regA[:, 0:4096].bitcast(F32)
            dec = regA[:, 4096:8192].bitcast(F32)
            skb = regA[:, 8192:10240].bitcast(F32)
            Hs = regA[:, 10240:14336].bitcast(F32)
            HsB = Buf()
            kb.dma("sp", rg_m, zT[0:33, :], Cd["zT"], r=[RB], w=[zB])
            kb.dma("sp", rg_m, w1[0:33, 0:64], Wd["b_filt_w1"][j], w=[B["tmpA"][0]])
            kb.dma("sp", rg_m, w2[0:64, 0:64], Wd["b_filt_w2"][j], w=[B["tmpA"][1]])
            kb.dma("sp", rg_m, w3[0:64, :], Wd["b_filt_w3"][j], r=B["regA"] + B["Y"], w=[w3B])
            kb.dma("sp", rg_m, dec, Wd["b_decay"][j].partition_broadcast(128), r=B["regA"] + B["Y"], w=[decB])
            kb.dma("sp", rg_m, skb, Wd["b_skip"][j].rearrange("o c -> (o c)").partition_broadcast(128),
                   r=B["regA"] + B["Y"], w=[skB])
            kb.dma("sp", rg_m, cols[0:64, 32:33], Wd["b_filt_b1"][j].rearrange("(p o) -> p o", o=1), w=[B["cols"]])
            kb.dma("sp", rg_m, cols[0:64, 33:34], Wd["b_filt_freq"][j, 0].rearrange("(p o) -> p o", o=1),
                   w=[B["cols"]])
            kb.dma("sp", rg_m, cols[0:64, 34:35], Wd["b_filt_b2"][j].rearrange("(p o) -> p o", o=1), w=[B["cols"]])
            kb.dma("sp", rg_m, cols[0:64, 35:36], Wd["b_filt_freq"][j, 1].rearrange("(p o) -> p o", o=1),
                   w=[B["cols"]])
            kb.op("dve", lambda e: e.tensor_tensor(out=cols[0:64, 36:37], in0=cols[0:64, 32:33], in1=cols[0:64, 33:34],
                                                   op=ALU.mult), r=[B["cols"]], w=[B["cols"]])
            kb.op("dve", lambda e: e.tensor_tensor(out=cols[0:64, 37:38], in0=cols[0:64, 34:35], in1=cols[0:64, 35:36],
                                                   op=ALU.mult), r=[B["cols"]], w=[B["cols"]])
            kb.op("act", lambda e: e.activation(out=dec, in_=dec, func=AF.Abs), r=[decB], w=[decB])

            def sin_layer(wt, kdim, src, srcB, dst, dstB, fcol, fbcol, wtB):
                for blk in range(4):
                    b = nextbank()
                    kb.op("pe", lambda e: e.matmul(bank(b)[0:64, :], lhsT=wt[0:kdim, 0:64],
                                                   rhs=src[0:kdim, blk * 512:(blk + 1) * 512], start=True, stop=True),
                          r=[wtB, srcB], w=[PB[b]])
                    a = arg[0:64, :]
                    m = msk[0:64, :]
                    kb.op("dve", lambda e: e.tensor_scalar(out=a, in0=bank(b)[0:64, :], scalar1=cols[0:64, fcol:fcol + 1],
                                                           scalar2=cols[0:64, fbcol:fbcol + 1], op0=ALU.mult,
                                                           op1=ALU.add), r=[PB[b], B["cols"]], w=[argB])
                    kb.op("dve", lambda e: e.tensor_scalar(out=m, in0=a, scalar1=PI, scalar2=-2 * PI, op0=ALU.is_gt,
                                                           op1=ALU.mult), r=[argB], w=[argB])
                    kb.op("dve", lambda e: e.tensor_tensor(out=a, in0=a, in1=m, op=ALU.add), r=[argB], w=[argB])
                    kb.op("dve", lambda e: e.tensor_scalar(out=m, in0=a, scalar1=-PI, scalar2=2 * PI, op0=ALU.is_lt,
                                                           op1=ALU.mult), r=[argB], w=[argB])
                    kb.op("dve", lambda e: e.tensor_tensor(out=a, in0=a, in1=m, op=ALU.add), r=[argB], w=[argB])
                    kb.op("act", lambda e: e.activation(out=dst[0:64, blk * 512:(blk + 1) * 512], in_=a, func=AF.Sin),
                          r=[argB], w=[dstB])

            sin_layer(w1, 33, zT, zB, h1, h1B, 33, 36, B["tmpA"][0])
            sin_layer(w2, 64, h1, h1B, h2, h2B, 35, 37, B["tmpA"][1])
            for n in range(NT):
                for cbk in range(4):
                    d_, o_ = cbk // 2, cbk % 2
                    b = nextbank()
                    kb.op("pe", lambda e: e.matmul(bank(b), lhsT=h2[0:64, n * 128:(n + 1) * 128],
                                                   rhs=w3[0:64, cbk * 512:(cbk + 1) * 512], start=True, stop=True),
                          r=[h2B, w3B], w=[PB[b]])
                    ti = cbk % 2
                    kb.op("act", lambda e: e.activation(out=tmpB[ti][:], in_=dec[:, cbk * 512:(cbk + 1) * 512],
                                                        func=AF.Exp, scale=negt01[:, n:n + 1]),
                          r=[decB, B["misc"]], w=[B["tmpB"][ti]])
                    kb.op("dve", lambda e: e.tensor_tensor(out=hbuf[:, d_ * 16 + n, o_ * 512:(o_ + 1) * 512],
                                                           in0=bank(b), in1=tmpB[ti][:], op=ALU.mult),
                          r=[PB[b], B["tmpB"][ti], RB], w=[hbB])
            HsBs = bufs(2)
            blocks = [(i, ri) for i in range(16) for ri in range(2)]
            loaded = {}

            def issue(bi):
                if bi < len(blocks):
                    i_, ri_ = blocks[bi]
                    fi_ = next_wb()
                    v_ = wb[fi_][:, :].rearrange("p (r f) -> p r f", r=32)
                    kb.dma("sp", rg_c, v_, Cd["fh"][i_ * 2 + ri_].rearrange("p (r f) -> p r f", r=32),
                           w=[B["wb"][fi_]])
                    loaded[bi] = (fi_, v_)

            issue(0)
            issue(1)
            for bi, (i, ri) in enumerate(blocks):
                fi, v = loaded.pop(bi)
                for o_ in range(2):
                    b = nextbank()
                    for rc in range(32):
                        kb.op("pe", lambda e: e.matmul(bank(b), lhsT=v[:, rc, :],
                                                       rhs=hbuf[:, rc, o_ * 512:(o_ + 1) * 512], start=(rc == 0),
                                                       stop=(rc == 31)),
                              r=[B["wb"][fi], hbB], w=[PB[b]], inc=(rc == 31))
                    dst = Hs[:, ri * 1024 + o_ * 512:ri * 1024 + (o_ + 1) * 512]
                    if ri == 0:
                        kb.op("dve", lambda e: e.tensor_tensor(out=dst, in0=bank(b),
                                                               in1=skb[:, o_ * 512:(o_ + 1) * 512], op=ALU.add),
                              r=[PB[b], skB], w=[HsBs[0]])
                    else:
                        kb.op("act", lambda e: e.copy(out=dst, in_=bank(b)), r=[PB[b]], w=[HsBs[1]])
                issue(bi + 2)
                kb.dma("sp", rg_st, Hd[j, i][:, ri * 1024:(ri + 1) * 1024], Hs[:, ri * 1024:(ri + 1) * 1024],
                       r=[HsBs[ri]], w=[HdB[j][i]])

        def mixer_even(s, layer, pre_norm=True, nxt=None, nst=None):
            j = layer // 2
            if pre_norm:
                norm_T(s, 0, layer)
            load_bcast(lnb, Wd["a_ln_g"][j], B["lnb"], 512)
            load_bcast(bsb, Wd["a_b_s"][j].rearrange("g p -> (g p)"), B["bsb"], 512)
            load_cols(Wd["b_conv_w"][j].rearrange("k (c p) -> (k c) p", p=128), 36, 0)
            kb.dma("pool", rg_w, wsN[:, :].rearrange("p (g q) -> p g q", g=4),
                   Wd["a_w_s"][j].rearrange("g p q -> p g q"), w=[B["wsN"]])
            b = nextbank()
            pT = bank(b).bitcast(BF16)[:, 0:512]
            for g in range(4):
                kb.op("pe", lambda e: e.transpose(out=pT[:, g * 128:(g + 1) * 128], in_=wsN[:, g * 128:(g + 1) * 128],
                                                  identity=ident[:]), r=[B["wsN"], B["ident"]], w=[PB[b]],
                      inc=(g == 3))
            kb.op("dve", lambda e: e.tensor_copy(out=wsT[:, :], in_=pT), r=[PB[b]], w=[B["wsT"]])
            wsv = wsT[:, :].rearrange("p (g q) -> p g q", g=4)
            bsv = bsb[:, :].rearrange("p (g q) -> p g q", g=4)
            win = Wd["ab_w_in"][j].rearrange("(k p) c -> p k c", p=128)
            yaT = Rv(0, 8192).rearrange("p (c t) -> p c t", c=4)
            zv = Rv(8192, 8192).rearrange("p (n c) -> p n c", n=16)
            x1 = Rv(16384, 8192).rearrange("p (n c) -> p n c", n=16)
            ybT = Rv(16384, 8192).rearrange("p (c t) -> p c t", c=4)
            x2T = Rv(24576, 8192).rearrange("p (c t) -> p c t", c=4)
            o = 32768
            raw = Rf(o, 2050); o += 4100
            acc = Rf(o, 2048); o += 4096
            pcb = Rv(o, 2048); o += 2048
            vg = Rf(o, 512); o += 1024
            vnb = Rv(o, 512); o += 512
            T1 = Rf(o, 512); o += 1024
            T2 = Rf(o, 512); o += 1024
            assert o <= 47104
            yaB, zvB, x1B, x2B, rawB, accB, pcbB, vgB, vnB, TB = bufs(10)
            kb.op("dve", lambda e: e.memset(raw[:, 0:1], 0.0), r=[RB], w=[rawB])
            kb.op("dve", lambda e: e.memset(raw[:, 2049:2050], 0.0), r=[RB], w=[rawB])
            zvT = bufs(16)
            x1T = bufs(16)
            ybB = Buf()
            w0, wt0 = get_wblock(("ein", layer, 0), win[:, :, 0:512])
            w1_, wt1 = get_wblock(("ein", layer, 1), win[:, :, 512:1024])
            for c in range(4):
                for tg in range(4):
                    b = nextbank()
                    for k in range(8):
                        kb.op("pe", lambda e: e.matmul(bank(b), lhsT=wt0[:, k, c * 128:(c + 1) * 128],
                                                       rhs=hT[:, k, 1 + tg * 512:1 + (tg + 1) * 512],
                                                       start=(k == 0), stop=(k == 7)),
                              r=[B["wb"][w0]] + B["regA"], w=[PB[b]], inc=(k == 7))
                    kb.op("act", lambda e: e.activation(out=yaT[:, c, tg * 512:(tg + 1) * 512], in_=bank(b),
                                                        func=GELU_FUNC), r=[PB[b], RB], w=[yaB])
            vgs = [vg, T1]
            vnbs = [vnb, T2.bitcast(BF16)[:, 0:512]]
            vgBs, vnBs = bufs(2), bufs(2)

            def a_stage1(n):
                v_, vn_, vB_, nB_ = vgs[n % 2], vnbs[n % 2], vgBs[n % 2], vnBs[n % 2]
                sc = 8 + (n % 2) * 4
                b = nextbank()
                for k in range(8):
                    kb.op("pe", lambda e: e.matmul(bank(b), lhsT=hT[:, k, 1 + n * 128:1 + (n + 1) * 128],
                                                   rhs=wt1[:, k, :], start=(k == 0), stop=(k == 7)),
                          r=[B["wb"][w1_], B["regA"][n]], w=[PB[b]], inc=(k == 7))
                kb.op("act", lambda e: e.activation(out=v_[:, :], in_=bank(b), func=GELU_FUNC), r=[PB[b], RB],
                      w=[vB_])
                kb.op("dve", lambda e: e.bn_stats(out=bn6[:, 0:6], in_=v_[:, :]), r=[vB_], w=[B["bn6"]])
                kb.op("dve", lambda e: e.bn_aggr(out=stat[:, sc:sc + 2], in_=bn6[:, 0:6]), r=[B["bn6"]],
                      w=[lnS[n % 2]])
                rstd_from_ss(stat[:, sc + 1:sc + 2], stat[:, sc + 2:sc + 3], stat[:, sc + 3:sc + 4], 1.0, 1e-5,
                             [lnS[n % 2]], [lnS[n % 2]])
                kb.op("dve", lambda e: e.tensor_scalar(out=v_[:, :], in0=v_[:, :], scalar1=stat[:, sc:sc + 1],
                                                       scalar2=stat[:, sc + 3:sc + 4], op0=ALU.subtract,
                                                       op1=ALU.mult), r=[lnS[n % 2]], w=[vB_])
                kb.op("dve", lambda e: e.tensor_tensor(out=vn_, in0=v_[:, :], in1=lnb[:, :], op=ALU.mult),
                      r=[vB_, B["lnb"]], w=[nB_])

            def a_stage2(n):
                vn_, nB_ = vnbs[n % 2], vnBs[n % 2]
                b2 = nextbank()
                sT = bank(b2).rearrange("p (g q) -> p g q", g=4)
                for g in range(4):
                    kb.op("pe", lambda e: e.matmul(sT[:, g, :], lhsT=vn_[:, g * 128:(g + 1) * 128], rhs=wsv[:, g, :],
                                                   start=True, stop=True), r=[nB_, B["wsT"]], w=[PB[b2]],
                          inc=(g == 3))
                ti = n % 2
                tv = tmpB[ti][:, :].rearrange("p (g q) -> p g q", g=4)
                kb.op("dve", lambda e: e.tensor_tensor(out=tv, in0=sT, in1=bsv, op=ALU.add),
                      r=[PB[b2], B["bsb"]], w=[B["tmpB"][ti]])
                kb.op("dve", lambda e: e.tensor_tensor(out=yaT[:, :, n * 128:(n + 1) * 128], in0=tv,
                                                       in1=yaT[:, :, n * 128:(n + 1) * 128], op=ALU.mult),
                      r=[B["tmpB"][ti]], w=[yaB])

            lnS = bufs(2)
            a_stage1(0)

            def a_step(n):
                if n + 1 < NT:
                    a_stage1(n + 1)
                a_stage2(n)
            a_steps = [(lambda n=n: a_step(n)) for n in range(NT)]
            pcbs = [pcb, pcb2[:, :]]
            pcbBs = [pcbB, Buf()]
            pend_t = []
            ci = 0
            for blk in range(3):
                if blk == 2:
                    while a_steps:
                        a_steps.pop(0)()
                wi = next_wb()
                wt = load_wblock(wi, win[:, :, 1024 + blk * 512:1024 + (blk + 1) * 512])
                for c in range(4):
                    cc = blk * 4 + c
                    for tg in range(4):
                        b = nextbank()
                        for k in range(8):
                            kb.op("pe", lambda e: e.matmul(bank(b), lhsT=wt[:, k, c * 128:(c + 1) * 128],
                                                           rhs=hT[:, k, 1 + tg * 512:1 + (tg + 1) * 512],
                                                           start=(k == 0), stop=(k == 7)),
                                  r=[B["wb"][wi]] + B["regA"], w=[PB[b]], inc=(k == 7))
                        kb.op("act", lambda e: e.copy(out=raw[:, 1 + tg * 512:1 + (tg + 1) * 512], in_=bank(b)),
                              r=[PB[b], RB], w=[rawB])
                    while pend_t:
                        pend_t.pop(0)()
                    kb.op("act", lambda e: e.activation(out=acc[:, :], in_=raw[:, 0:2048], func=AF.Copy,
                                                        scale=cols[:, cc:cc + 1]),
                          r=[rawB, B["cols"]], w=[accB])
                    kb.op("dve", lambda e: e.scalar_tensor_tensor(out=acc[:, :], in0=raw[:, 1:2049],
                                                                  scalar=cols[:, 12 + cc:13 + cc], in1=acc[:, :],
                                                                  op0=ALU.mult, op1=ALU.add),
                          r=[rawB, B["cols"]], w=[accB])
                    pc_ = pcbs[ci % 2]
                    pB_ = pcbBs[ci % 2]
                    ci += 1
                    dst = x2T[:, c, :] if blk == 2 else pc_
                    dB = x2B if blk == 2 else pB_
                    kb.op("dve", lambda e: e.scalar_tensor_tensor(out=dst, in0=raw[:, 2:2050],
                                                                  scalar=cols[:, 24 + cc:25 + cc], in1=acc[:, :],
                                                                  op0=ALU.mult, op1=ALU.add),
                          r=[rawB, accB, B["cols"]], w=[dB])
                    for _ in range(2):
                        if a_steps and blk < 2:
                            a_steps.pop(0)()
                    if blk < 2:
                        def tr(blk=blk, c=c, pc_=pc_, pB_=pB_):
                            tgt = zv if blk == 0 else x1
                            tB = zvT if blk == 0 else x1T
                            for h8 in range(2):
                                b = nextbank()
                                pT8 = bank(b).bitcast(BF16).rearrange("p (k t) -> p k t", k=8)
                                for q in range(8):
                                    n = h8 * 8 + q
                                    kb.op("pe", lambda e: e.transpose(out=pT8[:, q, :],
                                                                      in_=pc_[:, n * 128:(n + 1) * 128],
                                                                      identity=ident[:]), r=[pB_, B["ident"]],
                                          w=[PB[b]], inc=(q == 7))
                                kb.op("act", lambda e: e.copy(out=tgt[:, h8 * 8:(h8 + 1) * 8, c * 128:(c + 1) * 128],
                                                              in_=pT8), r=[PB[b]], w=tB[h8 * 8:(h8 + 1) * 8])
                        pend_t.append(tr)
            while pend_t:
                pend_t.pop(0)()
            def fwd_dft(order, srcT):
                for i in range(16):
                    fi = next_wb()
                    fv = wb[fi][:, :].rearrange("p (r s f) -> p r s f", r=2, s=16)
                    kb.dma("sp", rg_c, wb[fi][:, :], Cd["fz"][i], w=[B["wb"][fi]])
                    hi = i % 2
                    kb.dma("sp", rg_h, Hb[hi][:, :, :],
                           Hd[j, i].rearrange("p (r o c) -> p r o c", r=2, o=2)[:, :, order, :],
                           r=[HdB[j][i]], w=[B["Hb"][hi]])
                    bb = [nextbank(), nextbank()]
                    for ri in range(2):
                        for sc in range(16):
                            kb.op("pe", lambda e: e.matmul(bank(bb[ri]), lhsT=fv[:, ri, sc, :], rhs=zv[:, sc, :],
                                                           start=(sc == 0), stop=(sc == 15)),
                                  r=[B["wb"][fi], srcT[sc]], w=[PB[bb[ri]]], inc=(sc == 15))
                    zr, zi = bank(bb[0]), bank(bb[1])
                    hr, him = Hb[hi][:, 0, :], Hb[hi][:, 1, :]
                    rB = [PB[bb[0]], PB[bb[1]], B["Hb"][hi]]
                    kb.op("dve", lambda e: e.tensor_tensor(out=T1[:, :], in0=zr, in1=hr, op=ALU.mult), r=rB, w=[TB])
                    kb.op("dve", lambda e: e.tensor_tensor(out=T2[:, :], in0=zi, in1=him, op=ALU.mult), r=rB, w=[TB])
                    kb.op("dve", lambda e: e.tensor_tensor(out=Yv[:, i, :], in0=T1[:, :], in1=T2[:, :],
                                                           op=ALU.subtract), r=[TB], w=[B["Y"][i]])
                    kb.op("dve", lambda e: e.tensor_tensor(out=T1[:, :], in0=zr, in1=him, op=ALU.mult), r=rB, w=[TB])
                    kb.op("dve", lambda e: e.tensor_tensor(out=T2[:, :], in0=zi, in1=hr, op=ALU.mult), r=rB, w=[TB])
                    kb.op("dve", lambda e: e.tensor_tensor(out=Yv[:, 16 + i, :], in0=T1[:, :], in1=T2[:, :],
                                                           op=ALU.add), r=[TB], w=[B["Y"][16 + i]])

            def inv_dft(order):
                for tg in range(4):
                    bb = [nextbank() for _ in range(4)]
                    for fcg in range(4):
                        gi_ = next_wb()
                        gv = wb[gi_][:, :].rearrange("p (f t) -> p f t", f=8)
                        kb.dma("sp", rg_c, wb[gi_][:, :], Cd["ginv"][tg * 4 + fcg], w=[B["wb"][gi_]])
                        for fc in range(8):
                            fa = fcg * 8 + fc
                            for q in range(4):
                                st_, sp_ = (fa == 0), (fa == 31)
                                if order == 0:
                                    kb.op("pe", lambda e: e.matmul(bank(bb[q]), lhsT=gv[:, fc, q * 128:(q + 1) * 128],
                                                                   rhs=Yv[:, fa, :], start=st_, stop=sp_),
                                          r=[B["wb"][gi_], B["Y"][fa]], w=[PB[bb[q]]],
                                          inc=(sp_ or (fc == 7 and q == 3)))
                                else:
                                    kb.op("pe", lambda e: e.matmul(bank(bb[q]), lhsT=Yv[:, fa, q * 128:(q + 1) * 128],
                                                                   rhs=gv[:, fc, :], start=st_, stop=sp_),
                                          r=[B["wb"][gi_], B["Y"][fa]], w=[PB[bb[q]]],
                                          inc=(sp_ or (fc == 7 and q == 3)))
                    for q in range(4):
                        if order == 0:
                            n = tg * 4 + q
                            kb.op("dve", lambda e: e.tensor_tensor(out=zv[:, n, :], in0=bank(bb[q]), in1=x1[:, n, :],
                                                                   op=ALU.mult), r=[PB[bb[q]], x1T[n]], w=[zvT[n]])
                        else:
                            kb.op("dve", lambda e: e.tensor_tensor(out=ybT[:, q, tg * 512:(tg + 1) * 512],
                                                                   in0=bank(bb[q]),
                                                                   in1=x2T[:, q, tg * 512:(tg + 1) * 512],
                                                                   op=ALU.mult), r=[PB[bb[q]], x2B] + x1T, w=[ybB])

            fwd_dft(0, zvT)
            inv_dft(0)
            fwd_dft(1, zvT)
            inv_dft(1)
            out_proj(s, Wd["ab_w_out"][j], yaT, ybT, yaB, ybB, layer, nxt, nst)

        for st_ in plan:
            if st_[0] == "filt":
                filters(st_[1])
                kb.barrier()
        stages = [st_ for st_ in plan if st_[0] in ("even", "odd", "ffn")]
        xseq = {"done": False}
        for s in range(nseq):
            for k, st_ in enumerate(stages):
                nxt = None
                nst = None
                if k + 1 < len(stages):
                    nk = stages[k + 1]
                    nxt = (nk[1], 2 if nk[0] == "ffn" else 0)
                    nst = nk
                nsrc = None
                if k + 1 == len(stages) and s + 1 < nseq:
                    nst = stages[0]
                    if st_[0] == "ffn":
                        nxt = (nst[1], 2 if nst[0] == "ffn" else 0)
                        nsrc = s + 1
                pre_n = (k == 0) and not xseq["done"]
                xseq["done"] = nsrc is not None
                if st_[0] == "even":
                    mixer_even(s, st_[1], pre_n, nxt, nst)
                elif st_[0] == "odd":
                    mixer_odd(s, st_[1], pre_n, nxt, nst)
                elif st_[0] == "ffn":
                    ffn(s, st_[1], pre_n, nxt, nst, nsrc)
                kb.barrier()
        assert not pre, pre
        kb.finish()
    return nc


_NC_CACHE = {}


def run(inputs, nseq=SEQ_PER_CORE, plan=None, ncores=NCORES):
    key = (nseq, str(plan))
    if key not in _NC_CACHE:
        _NC_CACHE[key] = build(nseq, plan)
    nc = _NC_CACHE[key]
    C = _consts()
    x = np.ascontiguousarray(inputs["x"], dtype=np.float32)
    in_maps = []
    for c in range(ncores):
        m = {"x": x[c * nseq:(c + 1) * nseq]}
        for k in W_SPECS:
            m[k] = np.ascontiguousarray(inputs[k], dtype=np.float32)
        for k in CONST_SPECS:
            m[k] = C[k]
        in_maps.append(m)
    res = run_bass_kernel_spmd(nc, in_maps, core_ids=list(range(ncores)))
    return np.concatenate([res.results[c]["y"] for c in range(ncores)], axis=0)


def kernel(**inputs):
    return run(inputs).astype(np.float32)
```
